# Optimizing a Trainium2 kernel written in Bass

```python
import jax, jax.numpy as jnp
from jax import lax
import numpy as np

D_MODEL = 1024
BATCH = 8
SEQ = 4096
DEPTH = 4

CTX_LEN = 256
GRID_W = 64
N_MIXERS = 3
N_HEADS = 16
HEAD_DIM = D_MODEL // N_HEADS
ATTN_SCALE = HEAD_DIM ** -0.5
WIN_ROWS_MAX = 8
WIN_COLS = 16
RPB_ROWS = 2 * WIN_ROWS_MAX - 1
RPB_COLS = 2 * WIN_COLS - 1
CONF_KERNEL = 31
SHORT_KERNEL = 3
D_FF = 4 * D_MODEL
N_MOD = 6
EPS = 1e-6
N_NA = (DEPTH + 2) // 3
N_CONF = (DEPTH + 1) // 3
N_SC = DEPTH // 3

kernel_name = "hybrid_na_conformer_shortconv_dit"


def rmsnorm(x, g):
    xf = x.astype(jnp.float32)
    y = xf * lax.rsqrt(jnp.mean(xf * xf, axis=-1, keepdims=True) + EPS)
    return (y * g.astype(jnp.float32)).astype(x.dtype)


def layernorm(x, g, b):
    xf = x.astype(jnp.float32)
    mu = jnp.mean(xf, axis=-1, keepdims=True)
    xc = xf - mu
    var = jnp.mean(xc * xc, axis=-1, keepdims=True)
    y = xc * lax.rsqrt(var + EPS) * g.astype(jnp.float32) + b.astype(jnp.float32)
    return y.astype(x.dtype)


def modulate(h, shift, scale):
    return h * (1 + scale) + shift


def dwconv(x, w):
    k = w.shape[0]
    return lax.conv_general_dilated(
        x, w[:, None, :].astype(x.dtype), window_strides=(1,),
        padding=[((k - 1) // 2, k // 2)],
        dimension_numbers=("NWC", "WIO", "NWC"),
        feature_group_count=x.shape[-1])


def split_heads(t, n):
    b, l, _ = t.shape
    return t.reshape(b, l, n, N_HEADS, HEAD_DIM).transpose(2, 0, 3, 1, 4)


def neighbourhood_attention(h_lat, h_ctx, wqkv, wo, rpb, ctx_out):
    b, l, d = h_lat.shape
    rows = l // GRID_W
    kh = min(WIN_ROWS_MAX, rows)
    q, k, v = split_heads(h_lat @ wqkv, 3)
    q = q * ATTN_SCALE
    if ctx_out:
        q_c, k_c, v_c = split_heads(h_ctx @ wqkv, 3)
    else:
        k_c, v_c = split_heads(h_ctx @ wqkv[:, D_MODEL:], 2)
    q_grid = q.reshape(b, N_HEADS, rows, GRID_W, HEAD_DIM)
    k_grid = k.reshape(b, N_HEADS, rows, GRID_W, HEAD_DIM)
    v_grid = v.reshape(b, N_HEADS, rows, GRID_W, HEAD_DIM)

    r_ar = np.arange(rows)
    row_start = np.clip(r_ar - kh // 2, 0, rows - kh)
    row_off = row_start[:, None] + np.arange(kh)[None, :] - r_ar[:, None] + WIN_ROWS_MAX - 1
    c_ar = np.arange(GRID_W)
    col_idx = np.clip(c_ar - WIN_COLS // 2, 0, GRID_W - WIN_COLS)[:, None] + np.arange(WIN_COLS)[None, :]
    col_off = col_idx - c_ar[:, None] + WIN_COLS - 1
    bias_all = rpb[:, row_off[:, :, None, None], col_off[None, None, :, :]]
    bias_all = bias_all.transpose(1, 0, 3, 2, 4)
    n_lat = kh * WIN_COLS

    def row_block(xs):
        q_r, rs, bias_r = xs
        k_rows = lax.dynamic_slice_in_dim(k_grid, rs, kh, axis=2)
        v_rows = lax.dynamic_slice_in_dim(v_grid, rs, kh, axis=2)
        k_nb = k_rows[:, :, :, col_idx, :]
        v_nb = v_rows[:, :, :, col_idx, :]
        s_lat = jnp.einsum("bhqd,bhaqkd->bhqak", q_r, k_nb).astype(jnp.float32)
        s_lat = s_lat + bias_r.astype(jnp.float32)[None]
        s_ctx = jnp.einsum("bhqd,bhcd->bhqc", q_r, k_c).astype(jnp.float32)
        s = jnp.concatenate([s_lat.reshape(b, N_HEADS, GRID_W, n_lat), s_ctx], axis=-1)
        p = jax.nn.softmax(s, axis=-1).astype(v_nb.dtype)
        p_lat = p[..., :n_lat].reshape(b, N_HEADS, GRID_W, kh, WIN_COLS)
        p_ctx = p[..., n_lat:]
        return (jnp.einsum("bhqak,bhaqkd->bhqd", p_lat, v_nb)
                + jnp.einsum("bhqc,bhcd->bhqd", p_ctx, v_c))

    o_grid = lax.map(row_block, (jnp.moveaxis(q_grid, 2, 0),
                                 jnp.asarray(row_start, dtype=jnp.int32),
                                 bias_all))
    o_lat = o_grid.transpose(1, 0, 3, 2, 4).reshape(b, l, d) @ wo

    o_ctx = None
    if ctx_out:
        s_c = jnp.einsum("bhqd,bhkd->bhqk", q_c * ATTN_SCALE, k_c).astype(jnp.float32)
        p_c = jax.nn.softmax(s_c, axis=-1).astype(v_c.dtype)
        o_c = jnp.einsum("bhqk,bhkd->bhqd", p_c, v_c)
        o_ctx = o_c.transpose(0, 2, 1, 3).reshape(b, h_ctx.shape[1], d) @ wo
    return o_lat, o_ctx


def conformer_conv(h, w1, b1, dw, dwb, ln_g, ln_b, w2, b2):
    u = h @ w1 + b1
    a, g = jnp.split(u, 2, axis=-1)
    u = a * jax.nn.sigmoid(g)
    u = dwconv(u, dw) + dwb
    u = jax.nn.silu(layernorm(u, ln_g, ln_b))
    return u @ w2 + b2


def short_gated_conv(h, w_in, conv_w, w_out):
    bg, cg, v = jnp.split(h @ w_in, 3, axis=-1)
    return (bg * dwconv(cg * v, conv_w)) @ w_out


def sq_relu_mlp(h, w1, w2):
    return jnp.square(jax.nn.relu(h @ w1)) @ w2


def setup_inputs(seed: int = 0) -> dict:
    key = jax.random.key(seed)
    ks = jax.random.split(key, 32)
    nrm = jax.random.normal
    f32 = jnp.float32
    d = D_MODEL
    return {
        "x": nrm(ks[0], (BATCH, SEQ, d), f32),
        "c": nrm(ks[1], (BATCH, d), f32),
        "ctx": nrm(ks[2], (BATCH, CTX_LEN, d), f32),
        "c_ctx": nrm(ks[3], (d,), f32),
        "mod_w": nrm(ks[4], (DEPTH, d, N_MOD * d), f32) * (0.5 * d ** -0.5),
        "mod_b": nrm(ks[5], (DEPTH, N_MOD * d), f32) * 0.01,
        "norm1_g": 1.0 + 0.01 * nrm(ks[6], (DEPTH, d), f32),
        "norm2_g": 1.0 + 0.01 * nrm(ks[7], (DEPTH, d), f32),
        "mlp_w1": nrm(ks[8], (DEPTH, d, D_FF), f32) * d ** -0.5,
        "mlp_w2": nrm(ks[9], (DEPTH, D_FF, d), f32) * D_FF ** -0.5,
        "na_wqkv": nrm(ks[10], (N_NA, d, 3 * d), f32) * d ** -0.5,
        "na_wo": nrm(ks[11], (N_NA, d, d), f32) * d ** -0.5,
        "na_rpb": nrm(ks[12], (N_NA, N_HEADS, RPB_ROWS, RPB_COLS), f32) * 0.1,
        "cv_w1": nrm(ks[13], (N_CONF, d, 2 * d), f32) * d ** -0.5,
        "cv_b1": nrm(ks[14], (N_CONF, 2 * d), f32) * 0.01,
        "cv_dw": nrm(ks[15], (N_CONF, CONF_KERNEL, d), f32) * CONF_KERNEL ** -0.5,
        "cv_dwb": nrm(ks[16], (N_CONF, d), f32) * 0.01,
        "cv_ln_g": 1.0 + 0.01 * nrm(ks[17], (N_CONF, d), f32),
        "cv_ln_b": nrm(ks[18], (N_CONF, d), f32) * 0.01,
        "cv_w2": nrm(ks[19], (N_CONF, d, d), f32) * d ** -0.5,
        "cv_b2": nrm(ks[20], (N_CONF, d), f32) * 0.01,
        "sc_win": nrm(ks[21], (N_SC, d, 3 * d), f32) * d ** -0.5,
        "sc_conv": nrm(ks[22], (N_SC, SHORT_KERNEL, d), f32) * SHORT_KERNEL ** -0.5,
        "sc_wout": nrm(ks[23], (N_SC, d, d), f32) * d ** -0.5,
        "final_g": 1.0 + 0.01 * nrm(ks[24], (d,), f32),
    }


def reference(x, c, ctx, c_ctx, mod_w, mod_b, norm1_g, norm2_g, mlp_w1, mlp_w2,
              na_wqkv, na_wo, na_rpb, cv_w1, cv_b1, cv_dw, cv_dwb, cv_ln_g, cv_ln_b,
              cv_w2, cv_b2, sc_win, sc_conv, sc_wout, final_g):
    b = x.shape[0]
    silu_c = jax.nn.silu(c)
    silu_cc = jax.nn.silu(c_ctx)[None]
    last_attn = max(i for i in range(DEPTH) if i % N_MIXERS == 0)
    for i in range(DEPTH):
        kind = i % N_MIXERS
        slot = i // N_MIXERS
        ctx_live = i < last_attn
        need_ctx_in = ctx_live or kind == 0
        mod_l = (silu_c @ mod_w[i] + mod_b[i]).reshape(b, N_MOD, 1, D_MODEL)
        a_l = modulate(rmsnorm(x, norm1_g[i]), mod_l[:, 0], mod_l[:, 1])
        a_c = None
        if need_ctx_in:
            mod_c = (silu_cc @ mod_w[i] + mod_b[i]).reshape(1, N_MOD, 1, D_MODEL)
            a_c = modulate(rmsnorm(ctx, norm1_g[i]), mod_c[:, 0], mod_c[:, 1])
        if kind == 0:
            y_l, y_c = neighbourhood_attention(a_l, a_c, na_wqkv[slot], na_wo[slot],
                                               na_rpb[slot], ctx_live)
        elif kind == 1:
            cp = (cv_w1[slot], cv_b1[slot], cv_dw[slot], cv_dwb[slot], cv_ln_g[slot],
                  cv_ln_b[slot], cv_w2[slot], cv_b2[slot])
            y_l = conformer_conv(a_l, *cp)
            y_c = conformer_conv(a_c, *cp) if ctx_live else None
        else:
            sp = (sc_win[slot], sc_conv[slot], sc_wout[slot])
            y_l = short_gated_conv(a_l, *sp)
            y_c = short_gated_conv(a_c, *sp) if ctx_live else None
        x = x + mod_l[:, 2] * y_l
        m_l = modulate(rmsnorm(x, norm2_g[i]), mod_l[:, 3], mod_l[:, 4])
        x = x + mod_l[:, 5] * sq_relu_mlp(m_l, mlp_w1[i], mlp_w2[i])
        if ctx_live:
            ctx = ctx + mod_c[:, 2] * y_c
            m_c = modulate(rmsnorm(ctx, norm2_g[i]), mod_c[:, 3], mod_c[:, 4])
            ctx = ctx + mod_c[:, 5] * sq_relu_mlp(m_c, mlp_w1[i], mlp_w2[i])
    return rmsnorm(x, final_g)
```

```python
import numpy as np
import concourse.bass as bass
import concourse.mybir as mybir
from concourse.bass_utils import run_bass_kernel_spmd

F32 = mybir.dt.float32
BF16 = mybir.dt.bfloat16
AF = mybir.ActivationFunctionType
ALU = mybir.AluOpType

GRID_W = 64
WIN_ROWS = 8
WIN_COLS = 16
CONF_K = 31
SHORT_K = 3
EPS = 1e-6
NEG = -30000.0
SLOT = 4096
RING = 6
DEBUG = False


class Cfg:
    def __init__(self, D=1024, T=4096, C=256, depth=4, ncores=8):
        self.D, self.T, self.C, self.depth, self.ncores = D, T, C, depth, ncores
        self.DC = D // 128
        self.FF = 4 * D
        self.FC = self.FF // 128
        self.NH = D // 64
        self.NP = self.NH // 2
        self.TT = T + C
        self.rows = T // GRID_W
        self.NQB = self.rows // 2
        self.SC = min(SLOT // self.DC, D)
        self.NF = min(SLOT // D, self.FC)
        self.n_na = (depth + 2) // 3
        self.n_cv = (depth + 1) // 3
        self.n_sc = depth // 3
        assert self.SC >= 512 or True


_POS = {"matmul": ["out"], "memset": ["ap", "constant"], "iota": ["out", "pattern"]}


def _mk(meth, *a, **kw):
    names = _POS.get(meth, [])
    for nm, v in zip(names, a):
        kw[nm] = v
    assert len(a) <= len(names), (meth, len(a))
    return (meth, kw)


class Buf:
    __slots__ = ("name", "w", "r")

    def __init__(self, name=""):
        self.name = name
        self.w = None
        self.r = []


class Op:
    __slots__ = ("eng", "fn", "deps", "id", "sem", "ninc", "sig", "cnt")


class Prog:
    ENGS = ("pe", "act", "dve", "pool", "sp")

    def __init__(self, nc):
        self.nc = nc
        self.ops = []
        self.by_eng = {e: [] for e in self.ENGS}
        self.eng_sem = {e: nc.alloc_semaphore("es_" + e) for e in self.ENGS}
        self.last = {e: None for e in self.ENGS}
        self.last_sem = {}

    def new_sem(self, name):
        return self.nc.alloc_semaphore(name)

    def op(self, eng, fn, reads=(), writes=(), sem=None, ninc=1, extra=(), nobar=False):
        o = Op()
        o.eng, o.fn, o.id, o.sem, o.ninc, o.sig, o.cnt = eng, fn, len(self.ops), sem, ninc, False, None
        deps = set(extra)
        for b in reads:
            if b.w is not None:
                deps.add(b.w)
        for b in writes:
            if b.w is not None:
                deps.add(b.w)
            deps.update(b.r)
        for b in writes:
            b.w = o.id
            b.r = []
        for b in reads:
            b.r.append(o.id)
        deps.discard(o.id)
        o.deps = deps
        self.ops.append(o)
        self.by_eng[eng].append(o)
        if not nobar:
            self.last[eng] = o.id
        if sem is not None and not nobar:
            self.last_sem[id(sem)] = o.id
        return o.id

    def barrier(self):
        ids = [v for v in self.last.values() if v is not None] + list(self.last_sem.values())
        for e in self.ENGS:
            self.op(e, None, extra=ids)

    def call(self, e, fn):
        if callable(fn):
            return fn(e)
        if isinstance(fn, list):
            return [self.call(e, f) for f in fn]
        meth, kw = fn
        if kw.pop("_noncontig", False):
            with self.nc.allow_non_contiguous_dma(reason="small"):
                return getattr(e, meth)(**kw)
        return getattr(e, meth)(**kw)

    def lower(self, block):
        ops = self.ops
        for o in ops:
            for d in o.deps:
                p = ops[d]
                if p.eng == o.eng and p.eng == "pe" and p.sem is None:
                    continue
                p.sig = True
        semcnt = {}
        for o in ops:
            if o.fn is None:
                o.sig = False
            if o.sem is not None:
                o.sig = True
                c = semcnt.get(id(o.sem), 0) + 16 * o.ninc
                semcnt[id(o.sem)] = c
                o.cnt = (o.sem, c)
            elif o.sig:
                s = self.eng_sem[o.eng]
                c = semcnt.get(id(s), 0) + 1
                semcnt[id(s)] = c
                o.cnt = (s, c)
        prog = self

        def run(engname, e):
            waited = {}
            for o in prog.by_eng[engname]:
                need = {}
                for d in o.deps:
                    p = ops[d]
                    if p.cnt is None:
                        continue
                    s, c = p.cnt
                    if p.eng == engname and engname == "pe" and p.sem is None:
                        continue
                    k = id(s)
                    if c > need.get(k, (None, 0))[1]:
                        need[k] = (s, c)
                for k, (s, c) in need.items():
                    if c > waited.get(k, 0):
                        e.wait_ge(s, c)
                        waited[k] = c
                if o.fn is None:
                    continue
                res = prog.call(e, o.fn)
                if o.sem is not None:
                    insts = res if isinstance(res, (list, tuple)) else [res]
                    assert len(insts) == o.ninc, (len(insts), o.ninc)
                    for ins in insts:
                        ins.then_inc(o.sem, 16)
                elif o.sig:
                    res.then_inc(o.cnt[0], 1)

        @block.tensor
        def _(e):
            run("pe", e)

        @block.scalar
        def _(e):
            run("act", e)

        @block.vector
        def _(e):
            run("dve", e)

        @block.gpsimd
        def _(e):
            run("pool", e)

        @block.sync
        def _(e):
            run("sp", e)


def _vec(v):
    v = np.asarray(v, np.float32)
    return v.reshape(-1, 128).T


def build_vecs(cfg, inp):
    cols = []
    off = {}

    def add(name, arr):
        off[name] = sum(c.shape[1] for c in cols)
        cols.append(np.ascontiguousarray(arr, np.float32))

    for l in range(cfg.depth):
        add(f"n1g{l}", _vec(inp["norm1_g"][l]))
        add(f"n2g{l}", _vec(inp["norm2_g"][l]))
        add(f"modb{l}", _vec(inp["mod_b"][l]))
    add("fg", _vec(inp["final_g"]))
    add("ident", np.eye(128, dtype=np.float32))
    for s in range(cfg.n_cv):
        add(f"cvb1_{s}", _vec(inp["cv_b1"][s]))
        add(f"cvdwb_{s}", _vec(inp["cv_dwb"][s]))
        add(f"cvlng_{s}", _vec(inp["cv_ln_g"][s]))
        add(f"cvlnb_{s}", _vec(inp["cv_ln_b"][s]))
        add(f"cvb2_{s}", _vec(inp["cv_b2"][s]))
        add(f"cvdw_{s}", np.concatenate([_vec(inp["cv_dw"][s][k]) for k in range(CONF_K)], axis=1))
    for s in range(cfg.n_sc):
        add(f"sccv_{s}", np.concatenate([_vec(inp["sc_conv"][s][k]) for k in range(SHORT_K)], axis=1))
    return np.concatenate(cols, axis=1), off


def build_bias_tiles(rpb):
    nh = rpb.shape[0]
    kc = np.arange(GRID_W)[:, None]
    qc = np.arange(GRID_W)[None, :]
    cs = np.clip(qc - WIN_COLS // 2, 0, GRID_W - WIN_COLS)
    cmask = (kc >= cs) & (kc < cs + WIN_COLS)
    coff = np.clip(kc - qc + WIN_COLS - 1, 0, 2 * WIN_COLS - 2)

    def E(dr):
        if abs(dr) > WIN_ROWS - 1:
            return np.full((nh, GRID_W, GRID_W), NEG, np.float32)
        g = rpb[:, dr + WIN_ROWS - 1][:, coff]
        return np.where(cmask[None], g, np.float32(NEG)).astype(np.float32)

    def tile(delta, mask=None):
        t = np.full((nh, 128, 128), NEG, np.float32)
        for kh in range(2):
            for qh in range(2):
                if mask is not None and not mask(kh, qh):
                    continue
                t[:, kh * 64:(kh + 1) * 64, qh * 64:(qh + 1) * 64] = E(delta + kh - qh)
        return t

    tiles = [tile(d) for d in (-6, -4, -2, 0, 2, 4, 6)]
    tiles.append(tile(-4, lambda kh, qh: kh >= qh))
    tiles += [tile(d) for d in (-2, 0, 2)]
    tiles.append(tile(4, lambda kh, qh: kh == 0 and qh == 1))
    return np.ascontiguousarray(np.stack(tiles, axis=1))


def qb_plan(cfg, qb):
    n = cfg.NQB
    if qb == 0:
        return 0, 4, 3
    if qb == 1:
        return 0, 4, 2
    if qb == n - 2:
        return n - 4, 4, 1
    if qb == n - 1:
        return n - 4, 4, 0
    return qb - 2, 5, 7


def split_tiles(total, halo):
    k = -(-total // (512 - 2 * halo))
    base, rem = divmod(total, k)
    out, s = [], 0
    for i in range(k):
        n = base + (1 if i < rem else 0)
        out.append((s, n))
        s += n
    return out


def build_program(cfg, voff, nvec):
    D, DC, T, C, TT, FF, FC, NH, NP = cfg.D, cfg.DC, cfg.T, cfg.C, cfg.TT, cfg.FF, cfg.FC, cfg.NH, cfg.NP
    SC, NF = cfg.SC, cfg.NF
    depth = cfg.depth
    nc = bass.Bass("TRN2", target_bir_lowering=False)
    P = Prog(nc)

    def din(name, shape, dt=F32):
        return nc.dram_tensor(name, list(shape), dt, kind="ExternalInput").ap()

    xin = din("xin", [D, TT])
    ccin = din("cc", [D, 2])
    vecs_in = din("vecs", [128, nvec])
    mod_w = din("mod_w", [depth, D, 6 * D])
    mlp_w1 = din("mlp_w1", [depth, D, FF])
    mlp_w2 = din("mlp_w2", [depth, FF, D])
    na_wqkv = din("na_wqkv", [cfg.n_na, D, 3 * D])
    na_wo = din("na_wo", [cfg.n_na, D, D])
    nab = din("nab", [cfg.n_na, NH, 12, 128, 128])
    cv_w1 = din("cv_w1", [max(cfg.n_cv, 1), D, 2 * D])
    cv_w2 = din("cv_w2", [max(cfg.n_cv, 1), D, D])
    sc_win = din("sc_win", [max(cfg.n_sc, 1), D, 3 * D])
    sc_wout = din("sc_wout", [max(cfg.n_sc, 1), D, D])
    outT = nc.dram_tensor("outT", [D, T], F32, kind="ExternalOutput").ap()

    X = [nc.dram_tensor(f"xs{i}", [D, TT], F32).ap() for i in range(2)]
    ATd = nc.dram_tensor("attd", [D, TT], BF16).ap()
    DBG = {}
    if DEBUG:
        DBG["cv"] = nc.dram_tensor("dbg_cv", [D, TT], F32).ap()
        DBG["ln"] = nc.dram_tensor("dbg_ln", [D, TT], BF16).ap()
        DBG["sem"] = None

    slot_count = [0]

    def new_slots(n):
        s = slot_count[0]
        slot_count[0] += n
        return list(range(s, s + n))

    nsl_d = D // SC
    cat = []
    for l in range(depth):
        kind, sl = l % 3, l // 3
        e = {"kind": kind, "sl": sl}
        e["mod"] = new_slots(6 * D // SC)
        if kind == 0:
            e["qkv"] = new_slots(NP)
            e["wo"] = new_slots(nsl_d)
        elif kind == 1:
            e["w1"] = new_slots(DC // 2)
            e["dg"] = new_slots(DC)
            e["w2"] = new_slots(nsl_d)
        else:
            e["win"] = new_slots(DC)
            e["wout"] = new_slots(nsl_d)
        e["m1"] = new_slots(FF // SC)
        e["m2"] = new_slots(FC // NF)
        cat.append(e)
    NSLOT = slot_count[0]
    WS = nc.dram_tensor("wslots", [NSLOT, 128, SLOT], BF16).ap()

    def vcol(name, j=0, n=1):
        return (voff[name] + j, n)

    import contextlib
    es = contextlib.ExitStack()
    with es:
        def sb(name, shape, dt):
            return es.enter_context(nc.sbuf_tensor(name, list(shape), dt))

        vt = sb("vt", [128, nvec], F32)
        ones = sb("ones", [128, 128], BF16)
        ident = sb("ident", [128, 128], BF16)
        cct = sb("cct", [128, DC, 2], F32)
        sct = sb("sct", [128, DC, 2], F32)
        modv = sb("modv", [128, depth, 6 * DC, 2], F32)
        Avec = sb("Avec", [128, depth, 2, 2, DC], F32)
        c2v = sb("c2v", [128, 2, DC], F32)
        ring = sb("ring", [128, RING, SLOT], BF16)
        psum = [es.enter_context(nc.psum_tensor(f"ps{i}", [128, 512], F32)) for i in range(8)]
        psb = [Buf(f"ps{i}") for i in range(8)]
        ps_i = [0]

        pinned = set()

        def next_ps(pin=False):
            while True:
                i = ps_i[0] % 8
                ps_i[0] += 1
                if i not in pinned:
                    break
            if pin:
                pinned.add(i)
            return psum[i], psb[i]

        def V(name, j=0, n=1):
            o = voff[name] + j
            return vt[:, o:o + n]

        b_vt, b_ones, b_ident, b_sct, b_c2 = (Buf(n) for n in ("vt", "ones", "ident", "sct", "c2"))
        b_modv = [Buf(f"modv{i}") for i in range(depth)]
        b_avec = [Buf(f"avec{i}") for i in range(depth)]
        sctb = sb("sctb", [128, DC, 2], BF16)

        ring_b = [Buf(f"ring{i}") for i in range(RING)]
        ring_sem = [P.new_sem(f"rs{i}") for i in range(RING)]
        ring_q = []
        ring_state = {"emitted": 0, "used": 0}
        conv_op = {}
        slot_len = {}
        for e_ in cat:
            for k_, ln_ in (("mod", DC * SC), ("qkv", DC * 384), ("wo", DC * SC), ("w1", DC * 512), ("dg", CONF_K * 128), ("w2", DC * SC),
                            ("win", DC * 384), ("wout", DC * SC), ("m1", DC * SC), ("m2", NF * D)):
                for s_ in e_.get(k_, []):
                    slot_len[s_] = ln_

        ring_hold = {}

        def ring_emit(v):
            while ring_state["emitted"] < min(v + RING, len(ring_q)):
                x = ring_state["emitted"]
                w_ = x - RING
                if w_ >= 0 and v < w_ + ring_hold.get(w_, 0) + 1:
                    break
                sid = ring_q[x]
                pos = x % RING
                ln = slot_len[sid]
                P.op("sp", _mk("dma_start", out=ring[:, pos, 0:ln], in_=WS[sid][:, 0:ln]),
                     writes=[ring_b[pos]], sem=ring_sem[pos], extra=[conv_op[sid]])
                ring_state["emitted"] += 1

        def ring_get(sid, hold=0):
            u = ring_state["used"]
            assert ring_q[u] == sid, (u, ring_q[u], sid)
            ring_hold[u] = hold
            ring_emit(u)
            assert ring_state["emitted"] > u
            ring_state["used"] += 1
            pos = u % RING
            return ring[:, pos, :], ring_b[pos]

        c_sem = P.new_sem("csem")
        P.op("sp", lambda e: e.dma_start(out=vt[:, :], in_=vecs_in[:, :]), writes=[b_vt], sem=c_sem)
        b_cc = Buf("cc")

        def ld_cc(e):
            with nc.allow_non_contiguous_dma(reason="tiny"):
                return e.dma_start(out=cct[:, :, :], in_=ccin.rearrange("(k p) t -> p k t", p=128))
        P.op("sp", ld_cc, writes=[b_cc], sem=P.new_sem("csem2"))
        P.op("dve", lambda e: e.memset(ones[:, :], 1.0), writes=[b_ones])
        P.op("dve", _mk("tensor_copy", out=ident[:, :], in_=V("ident", 0, 128)), reads=[b_vt], writes=[b_ident])

        def conv_group(name, parts):
            sem = P.new_sem("cg_" + name)

            def fn(e, parts=parts):
                return [e.dma_start(out=d, in_=s) for d, s in parts]
            return P.op("pool", fn, sem=sem, ninc=len(parts), nobar=True)

        def parts_kd(slots, src2d):
            return [(WS[sid][:, 0:DC * SC].rearrange("p (k n) -> p k n", k=DC),
                     src2d[:, i * SC:(i + 1) * SC].rearrange("(k p) n -> p k n", p=128)) for i, sid in enumerate(slots)]

        def parts_blocks(slots, src2d, blocks):
            r = []
            for sid, blks in zip(slots, blocks):
                nb = len(blks)
                dst = WS[sid][:, 0:DC * nb * 128].rearrange("p (k n) -> p k n", k=DC)
                for bi, c0 in enumerate(blks):
                    r.append((dst[:, :, bi * 128:(bi + 1) * 128],
                              src2d[:, c0:c0 + 128].rearrange("(k p) n -> p k n", p=128)))
            return r

        def parts_w2(slots, src2d):
            return [(WS[sid][:, 0:NF * D].rearrange("p (f n) -> p f n", f=NF),
                     src2d[i * NF * 128:(i + 1) * NF * 128, :].rearrange("(f p) n -> p f n", p=128))
                    for i, sid in enumerate(slots)]

        def reg(slots, opid):
            for s in slots:
                conv_op[s] = opid

        dgs_sem = [P.new_sem("dg0"), P.new_sem("dg1")]

        def emit_convs(l):
            e = cat[l]
            kind, sl = e["kind"], e["sl"]
            reg(e["mod"], conv_group(f"mod{l}", parts_kd(e["mod"], mod_w[l])))
            if kind == 0:
                blocks = [[j * 128, D + j * 128, 2 * D + j * 128] for j in range(NP)]
                reg(e["qkv"], conv_group(f"qkv{l}", parts_blocks(e["qkv"], na_wqkv[sl], blocks)))
                reg(e["wo"], conv_group(f"wo{l}", parts_kd(e["wo"], na_wo[sl])))
            elif kind == 1:
                blocks = [[2 * s * 128, D + 2 * s * 128, (2 * s + 1) * 128, D + (2 * s + 1) * 128] for s in range(DC // 2)]
                reg(e["w1"], conv_group(f"cw1{l}", parts_blocks(e["w1"], cv_w1[sl], blocks)))
                reg(e["w2"], conv_group(f"cw2{l}", parts_kd(e["w2"], cv_w2[sl])))
            else:
                blocks = [[c * 128, D + c * 128, 2 * D + c * 128] for c in range(DC)]
                reg(e["win"], conv_group(f"swi{l}", parts_blocks(e["win"], sc_win[sl], blocks)))
                reg(e["wout"], conv_group(f"swo{l}", parts_kd(e["wout"], sc_wout[sl])))
            reg(e["m1"], conv_group(f"m1{l}", parts_kd(e["m1"], mlp_w1[l])))
            reg(e["m2"], conv_group(f"m2{l}", parts_w2(e["m2"], mlp_w2[l])))

        def emit_diag(l, alloc):
            e = cat[l]
            kind, sl = e["kind"], e["sl"]
            if kind == 1:
                dgs = alloc(f"dgs{l}", [128, 2, CONF_K * 128], BF16)
                dgs_b = [Buf("dgs0"), Buf("dgs1")]
                for c in range(DC):
                    s = c % 2
                    for k in range(CONF_K):
                        P.op("dve", _mk("tensor_scalar",
                            out=dgs[:, s, k * 128:(k + 1) * 128], in0=ident[:, :],
                            scalar1=V(f"cvdw_{sl}", k * DC + c, 1), scalar2=None, op0=ALU.mult),
                             reads=[b_ident, b_vt], writes=[dgs_b[s]])
                    sid = e["dg"][c]
                    conv_op[sid] = P.op("sp", _mk("dma_start",
                        out=WS[sid][:, 0:CONF_K * 128], in_=dgs[:, s, :]), reads=[dgs_b[s]], sem=dgs_sem[s])

        def emit_mod(l):
            e = cat[l]
            ring_q.extend(e["mod"])
            pt, pb = next_ps()
            pv = pt[:, 0:6 * DC * 2].rearrange("p (c t) -> p c t", t=2)
            bps = SC // 128
            for si, sid in enumerate(e["mod"]):
                rt, rb = ring_get(sid)
                for j in range(bps):
                    cb = si * bps + j
                    for kc in range(DC):
                        P.op("pe", _mk("matmul",
                            pv[:, cb, :], lhsT=rt[:, kc * SC + j * 128: kc * SC + (j + 1) * 128], rhs=sctb[:, kc, :],
                            start=(kc == 0), stop=(kc == DC - 1)), reads=[rb, b_sct], writes=[pb])
            for wh in range(2):
                P.op("dve", _mk("tensor_tensor",
                    out=modv[:, l, :, wh], in0=pv[:, :, wh], in1=V(f"modb{l}", 0, 6 * DC), op=ALU.add),
                     reads=[pb, b_vt], writes=[b_modv[l]])
            for wh in range(2):
                for nm, (gname, mi) in enumerate((("n1g", 1), ("n2g", 4))):
                    P.op("dve", _mk("scalar_tensor_tensor",
                        out=Avec[:, l, wh, nm, :], in0=modv[:, l, mi * DC:(mi + 1) * DC, wh], scalar=1.0,
                        in1=V(f"{gname}{l}", 0, DC), op0=ALU.add, op1=ALU.mult),
                         reads=[b_modv[l], b_vt], writes=[b_avec[l]])
            if cat[l]["kind"] == 1:
                sl = cat[l]["sl"]
                for wh in range(2):
                    P.op("dve", _mk("tensor_tensor",
                        out=c2v[:, wh, :], in0=modv[:, l, 2 * DC:3 * DC, wh], in1=V(f"cvb2_{sl}", 0, DC),
                        op=ALU.mult), reads=[b_modv[l], b_vt], writes=[b_c2])

        P.op("act", lambda e: e.activation(out=sct[:, :, :], in_=cct[:, :, :], func=AF.Silu),
             reads=[b_cc], writes=[b_sct])
        P.op("act", lambda e: e.activation(out=sctb[:, :, :], in_=sct[:, :, :], func=AF.Copy),
             reads=[b_sct], writes=[b_sct])
        emit_convs(0)
        emit_diag(0, sb)
        emit_mod(0)

        def mod_col(l, m, kc, wh):
            return modv[:, l, m * DC + kc, wh:wh + 1]

        def norm_mod(l, wh, nm, xt, xb, lo, hi, out_fn, out_bufs, st, sq_pool=False):
            s_ = norm_stats(xt, xb, lo, hi, st, sq_pool)
            norm_apply(l, wh, nm, xt, xb, lo, hi, out_fn, out_bufs, st, s_)

        def norm_stats(xt, xb, lo, hi, st, sq_pool=False):
            w = hi - lo
            pt, pb = next_ps()
            for kc in range(DC):
                s = st["sqi"] % 4
                st["sqi"] += 1
                if sq_pool and kc % 2 == 1:
                    P.op("pool", _mk("tensor_tensor", out=st["sq"][:, s, 0:w], in0=xt[:, kc, lo:hi],
                                     in1=xt[:, kc, lo:hi], op=ALU.mult), reads=[xb[kc]], writes=[st["sqb"][s]])
                else:
                    P.op("act", _mk("activation", out=st["sq"][:, s, 0:w], in_=xt[:, kc, lo:hi],
                                                              func=AF.Square),
                         reads=[xb[kc]], writes=[st["sqb"][s]])
                P.op("pe", _mk("matmul", pt[:, 0:w], lhsT=ones[:, :], rhs=st["sq"][:, s, 0:w],
                                                          start=(kc == 0), stop=(kc == DC - 1)),
                     reads=[st["sqb"][s], b_ones], writes=[pb])
            r = st["rsi"] % 2
            st["rsi"] += 1
            rs, rsb = st["rs"], st["rsb"][r]
            P.op("act", _mk("activation", out=rs[:, r, 0:w], in_=pt[:, 0:w], func=AF.Ln, scale=1.0 / D,
                                                bias=st["epsc"][:, 0:1]), reads=[pb, st["epsb"]], writes=[rsb])
            P.op("act", _mk("activation", out=pt[:, 0:w], in_=rs[:, r, 0:w], func=AF.Exp, scale=-0.5),
                 reads=[rsb], writes=[pb])
            return pt, pb

        def norm_apply(l, wh, nm, xt, xb, lo, hi, out_fn, out_bufs, st, s_):
            pt, pb = s_
            w = hi - lo
            for kc in range(DC):
                s = st["tmi"] % 2
                st["tmi"] += 1
                if nm is None:
                    P.op("dve", _mk("scalar_tensor_tensor",
                        out=out_fn(kc), in0=xt[:, kc, lo:hi], scalar=V("fg", kc, 1), in1=pt[:, 0:w],
                        op0=ALU.mult, op1=ALU.mult), reads=[xb[kc], pb, b_vt], writes=[out_bufs[kc]])
                    continue
                P.op("dve", _mk("scalar_tensor_tensor",
                    out=st["tm"][:, s, 0:w], in0=xt[:, kc, lo:hi], scalar=Avec[:, l, wh, nm, kc:kc + 1],
                    in1=pt[:, 0:w], op0=ALU.mult, op1=ALU.mult),
                     reads=[xb[kc], pb, b_avec[l]], writes=[st["tmb"][s]])
                P.op("act", _mk("activation",
                    out=out_fn(kc), in_=st["tm"][:, s, 0:w], func=AF.Identity,
                    bias=mod_col(l, 0 if nm == 0 else 3, kc, wh), scale=1.0),
                     reads=[st["tmb"][s], b_modv[l]], writes=[out_bufs[kc]])

        def proj_kd(slots, rhs_fn, rhs_bufs, w, evac):
            bps = SC // 128
            for si, sid in enumerate(slots):
                rt, rb = ring_get(sid)
                grp = [next_ps() for _ in range(bps)]
                for kc in range(DC):
                    for j in range(bps):
                        pt, pb = grp[j]
                        P.op("pe", _mk("matmul",
                            pt[:, 0:w], lhsT=rt[:, kc * SC + j * 128: kc * SC + (j + 1) * 128], rhs=rhs_fn(kc),
                            start=(kc == 0), stop=(kc == DC - 1)), reads=[rb, rhs_bufs[kc]], writes=[pb])
                for j in range(bps):
                    evac(si * bps + j, grp[j][0], grp[j][1])

        def mlp(l, wh, mt, mtb, w, xt, xb, lo, st):
            hT, hb = st["hT"], st["hb"]
            e = cat[l]
            bps = SC // 128
            for si, sid in enumerate(e["m1"]):
                rt, rb = ring_get(sid)
                grp = [next_ps() for _ in range(bps)]
                for kc in range(DC):
                    for j in range(bps):
                        pt, pb = grp[j]
                        P.op("pe", _mk("matmul",
                            pt[:, 0:w], lhsT=rt[:, kc * SC + j * 128: kc * SC + (j + 1) * 128], rhs=mt[:, kc, 0:w],
                            start=(kc == 0), stop=(kc == DC - 1)), reads=[rb, mtb[kc]], writes=[pb])
                for j in range(bps):
                    fc = si * bps + j
                    pt, pb = grp[j]
                    s = st["rli"] % 2
                    st["rli"] += 1
                    P.op("act", _mk("activation", out=st["rl"][:, s, 0:w], in_=pt[:, 0:w],
                                                                    func=AF.Relu),
                         reads=[pb], writes=[st["rlb"][s]])
                    P.op("dve", _mk("tensor_tensor",
                        out=hT[:, fc, 0:w], in0=st["rl"][:, s, 0:w], in1=st["rl"][:, s, 0:w], op=ALU.mult),
                         reads=[st["rlb"][s]], writes=[hb[fc]])
            accs = [next_ps() for _ in range(DC)]
            for si, sid in enumerate(e["m2"]):
                rt, rb = ring_get(sid)
                for f in range(NF):
                    fc = si * NF + f
                    for dc in range(DC):
                        pt, pb = accs[dc]
                        P.op("pe", _mk("matmul",
                            pt[:, 0:w], lhsT=rt[:, f * D + dc * 128: f * D + (dc + 1) * 128], rhs=hT[:, fc, 0:w],
                            start=(fc == 0), stop=(fc == FC - 1)), reads=[rb, hb[fc]], writes=[pb])
            for dc in range(DC):
                pt, pb = accs[dc]
                P.op("dve", _mk("scalar_tensor_tensor",
                    out=xt[:, dc, lo:lo + w], in0=pt[:, 0:w], scalar=mod_col(l, 5, dc, wh), in1=xt[:, dc, lo:lo + w],
                    op0=ALU.mult, op1=ALU.add), reads=[pb, b_modv[l], xb[dc]], writes=[xb[dc]])

        def resid_evac(l, wh, xt, xb, lo, w, add_c2=False):
            def ev(oc, pt, pb):
                P.op("dve", _mk("scalar_tensor_tensor",
                    out=xt[:, oc, lo:lo + w], in0=pt[:, 0:w], scalar=mod_col(l, 2, oc, wh), in1=xt[:, oc, lo:lo + w],
                    op0=ALU.mult, op1=ALU.add), reads=[pb, b_modv[l], xb[oc]], writes=[xb[oc]])
                if add_c2:
                    P.op("dve", _mk("tensor_scalar",
                        out=xt[:, oc, lo:lo + w], in0=xt[:, oc, lo:lo + w], scalar1=c2v[:, wh, oc:oc + 1], scalar2=None,
                        op0=ALU.add), reads=[xb[oc], b_c2], writes=[xb[oc]])
            return ev

        def make_tiles(halo, with_ctx):
            tl = []
            for (s, n) in split_tiles(T, halo):
                tl.append((0, s, n, min(halo, s), min(halo, T - s - n)))
            if with_ctx:
                for (s, n) in split_tiles(C, halo):
                    tl.append((1, T + s, n, min(halo, s), min(halo, C - s - n)))
            return tl

        last_attn_ = max(i for i in range(depth) if i % 3 == 0)

        def ring_seq_main(l):
            e = cat[l]
            kind = e["kind"]
            live = l < last_attn_
            seq = []
            if kind != 0:
                halo_ = {1: CONF_K // 2, 2: SHORT_K // 2}[kind]
                for _t in make_tiles(halo_, live):
                    if kind == 1:
                        seq.append(e["w1"][0])
                        for c in range(DC):
                            if c + 1 < DC and (c + 1) % 2 == 0:
                                seq.append(e["w1"][(c + 1) // 2])
                            seq.append(e["dg"][c])
                        seq.extend(e["w2"])
                    else:
                        seq.extend(e["win"])
                        seq.extend(e["wout"])
                    seq.extend(e["m1"])
                    seq.extend(e["m2"])
            else:
                seq.extend(e["qkv"])
                for _t in make_tiles(0, live):
                    seq.extend(e["wo"])
                    seq.extend(e["m1"])
                    seq.extend(e["m2"])
            return seq

        ring_q.extend(ring_seq_main(0))
        x_sem = [P.new_sem("xs0"), P.new_sem("xs1")]
        st_sem = [P.new_sem("st0"), P.new_sem("st1")]

        src = xin
        for l in range(depth):
            e = cat[l]
            kind, sl = e["kind"], e["sl"]
            last = (l == depth - 1)
            last_attn = max(i for i in range(depth) if i % 3 == 0)
            ctx_live = l < last_attn
            dst = X[l % 2]
            with contextlib.ExitStack() as ls:
                cur = [ls]
                uniq = [0]

                def lsb(name, shape, dt):
                    uniq[0] += 1
                    return cur[0].enter_context(nc.sbuf_tensor(f"{name}_{l}_{uniq[0]}", list(shape), dt))

                halo = {0: 0, 1: CONF_K // 2, 2: SHORT_K // 2}[kind]

                if l + 1 < depth and kind != 0:
                    emit_convs(l + 1)

                def prep_next():
                    if l + 1 < depth:
                        if kind == 0:
                            emit_convs(l + 1)
                        emit_diag(l + 1, lsb)

                def finish_next():
                    if l + 1 < depth:
                        emit_mod(l + 1)
                        ring_q.extend(ring_seq_main(l + 1))
                        ring_emit(ring_state["used"])

                def alloc_common(with_fo=False):
                    st_ = {"sqi": 0, "rsi": 0, "tmi": 0, "rli": 0}
                    st_["sq"] = lsb("sq", [128, 4, 512], BF16)
                    st_["sqb"] = [Buf() for _ in range(4)]
                    st_["rs"] = lsb("rs", [128, 2, 512], F32)
                    st_["rsb"] = [Buf(), Buf()]
                    st_["tm"] = lsb("tm", [128, 2, 512], F32)
                    st_["tmb"] = [Buf(), Buf()]
                    st_["rl"] = lsb("rl", [128, 2, 512], F32)
                    st_["rlb"] = [Buf(), Buf()]
                    st_["epsc"] = lsb("epsc", [128, 1], F32)
                    st_["epsb"] = Buf()
                    P.op("dve", _mk("memset", st_["epsc"][:, :], EPS), writes=[st_["epsb"]])
                    xts_ = lsb("xt", [128, 2, DC, 512], F32)
                    xtb_ = [[Buf() for _ in range(DC)] for _ in range(2)]
                    mT_ = lsb("mT", [128, DC, 512], BF16)
                    mTb_ = [Buf() for _ in range(DC)]
                    if with_fo:
                        st_["fo"] = lsb("fo", [128, DC, 512], F32)
                        st_["fob"] = [Buf() for _ in range(DC)]
                        st_["fosem"] = P.new_sem(f"fosem{uniq[0]}")
                    return st_, xts_, xtb_, mT_, mTb_

                if kind != 0:
                    st, xts, xtb, mT, mTb = alloc_common(last)

                def load_x(slot, col0, ncols, lo):
                    P.op("sp", _mk("dma_start",
                        out=xts[:, slot, :, lo:lo + ncols],
                        in_=src.rearrange("(k p) t -> p k t", p=128)[:, :, col0:col0 + ncols]),
                         writes=xtb[slot], sem=x_sem[slot])

                def store_x(slot, col0, ncols, lo, dram):
                    P.op("sp", _mk("dma_start",
                        out=dram.rearrange("(k p) t -> p k t", p=128)[:, :, col0:col0 + ncols],
                        in_=xts[:, slot, :, lo:lo + ncols]), reads=xtb[slot], sem=st_sem[slot])

                def tail(wh, slot, col0, n, lo):
                    xt = xts[:, slot]
                    norm_mod(l, wh, 1, xt, xtb[slot], lo, lo + n, lambda kc: mT[:, kc, 0:n], mTb, st)
                    mlp(l, wh, mT, mTb, n, xt, xtb[slot], lo, st)
                    if last:
                        fo = st["fo"]
                        norm_mod(l, wh, None, xt, xtb[slot], lo, lo + n, lambda kc: fo[:, kc, 0:n], st["fob"], st)
                        P.op("sp", _mk("dma_start",
                            out=outT.rearrange("(k p) t -> p k t", p=128)[:, :, col0:col0 + n], in_=fo[:, :, 0:n]),
                             reads=st["fob"], sem=st["fosem"])
                    else:
                        store_x(slot, col0, n, lo, dst)

                if kind != 0:
                    st["hT"] = lsb("hT", [128, FC, 512], BF16)
                    st["hb"] = [Buf() for _ in range(FC)]
                    tiles = make_tiles(halo, ctx_live)
                    aT = lsb("aT", [128, DC, 512], BF16)
                    aTb = [Buf() for _ in range(DC)]
                    if kind == 1:
                        sg = lsb("sg", [128, 2, 512], F32)
                        sgb = [Buf(), Buf()]
                        ub = lsb("ub", [128, 2, 512], BF16)
                        ubb = [Buf(), Buf()]
                        cvf = lsb("cvf", [128, DC, 512], F32)
                        cvb = [Buf() for _ in range(DC)]
                        vb = lsb("vb", [128, 2, 2, 512], BF16)
                        vbb = [Buf(), Buf()]
                        lnm = lsb("lnm", [128, 4, 512], F32)
                        lnmb = [Buf() for _ in range(4)]
                        t2 = lsb("t2", [128, 2, 512], F32)
                        t2b = [Buf(), Buf()]
                        lno = lsb("lno", [128, DC, 512], BF16)
                        lnob = [Buf() for _ in range(DC)]
                    else:
                        cgs = lsb("cgs", [128, 2, 512], F32)
                        cgb = [Buf(), Buf()]
                        zt = lsb("zt", [128, 2, 512], F32)
                        ztb = [Buf(), Buf()]
                        acc = lsb("acc", [128, 2, 512], F32)
                        accb = [Buf(), Buf()]
                        gbt = lsb("gbt", [128, DC, 512], BF16)
                        gbb = [Buf() for _ in range(DC)]
                    prep_next()
                    H = halo

                    def issue_load(ti):
                        wh, c0, n, hl, hr = tiles[ti]
                        load_x(ti % 2, c0 - hl, n + hl + hr, H - hl)
                    issue_load(0)
                    ci = [0]
                    for ti, (wh, c0, n, hl, hr) in enumerate(tiles):
                        slot = ti % 2
                        if ti + 1 < len(tiles):
                            issue_load(ti + 1)
                        xt = xts[:, slot]
                        lo, hi = H - hl, H + n + hr
                        w = hi - lo
                        norm_mod(l, wh, 0, xt, xtb[slot], lo, hi, lambda kc: aT[:, kc, lo:hi], aTb, st)
                        if kind == 1:
                            s1t, s1b = next_ps(pin=True)
                            s2t, s2b = next_ps(pin=True)
                            w1state = {}

                            def stA(c):
                                s, cc = divmod(c, 2)
                                if cc == 0:
                                    w1state["rt"], w1state["rb"] = ring_get(e["w1"][s], hold=2)
                                rt, rb = w1state["rt"], w1state["rb"]
                                pa, pab = next_ps()
                                pg, pgb = next_ps()
                                for kc in range(DC):
                                    for (pt, pb, blk) in ((pa, pab, 2 * cc), (pg, pgb, 2 * cc + 1)):
                                        P.op("pe", _mk("matmul",
                                            pt[:, lo:hi], lhsT=rt[:, kc * 512 + blk * 128: kc * 512 + (blk + 1) * 128],
                                            rhs=aT[:, kc, lo:hi], start=(kc == 0), stop=(kc == DC - 1)),
                                             reads=[rb, aTb[kc]], writes=[pb])
                                k2 = ci[0] % 2
                                ci[0] += 1
                                P.op("act", _mk("activation",
                                    out=sg[:, k2, lo:hi], in_=pg[:, lo:hi], func=AF.Sigmoid,
                                    bias=V(f"cvb1_{sl}", DC + c, 1), scale=1.0), reads=[pgb, b_vt], writes=[sgb[k2]])
                                if hl < H or hr < H:
                                    P.op("dve", _mk("memset", ub[:, k2, :], 0.0), writes=[ubb[k2]])
                                P.op("dve", _mk("scalar_tensor_tensor",
                                    out=ub[:, k2, lo:hi], in0=pa[:, lo:hi], scalar=V(f"cvb1_{sl}", c, 1),
                                    in1=sg[:, k2, lo:hi], op0=ALU.add, op1=ALU.mult),
                                     reads=[pab, sgb[k2], b_vt], writes=[ubb[k2]])
                                return k2

                            def stB(c, k2):
                                dt_, db_ = ring_get(e["dg"][c])
                                pc, pcb = next_ps()
                                for k in range(CONF_K):
                                    P.op("pe", _mk("matmul",
                                        pc[:, 0:n], lhsT=dt_[:, k * 128:(k + 1) * 128], rhs=ub[:, k2, k:k + n],
                                        start=(k == 0), stop=(k == CONF_K - 1)), reads=[db_, ubb[k2]], writes=[pcb])
                                P.op("act", _mk("activation",
                                    out=cvf[:, c, 0:n], in_=pc[:, 0:n], func=AF.Identity,
                                    bias=V(f"cvdwb_{sl}", c, 1), scale=1.0), reads=[pcb, b_vt], writes=[cvb[c]])
                                P.op("act", _mk("activation",
                                    out=vb[:, k2, 0, 0:n], in_=pc[:, 0:n], func=AF.Identity,
                                    bias=V(f"cvdwb_{sl}", c, 1), scale=1.0), reads=[pcb, b_vt], writes=[vbb[k2]])
                                P.op("act", _mk("activation",
                                    out=vb[:, k2, 1, 0:n], in_=pc[:, 0:n], func=AF.Square,
                                    bias=V(f"cvdwb_{sl}", c, 1), scale=1.0), reads=[pcb, b_vt, vbb[k2]], writes=[vbb[k2]])

                            def stC(c, k2):
                                for q, (stt_, stb_) in enumerate(((s1t, s1b), (s2t, s2b))):
                                    P.op("pe", _mk("matmul",
                                        stt_[:, 0:n], lhsT=ones[:, :], rhs=vb[:, k2, q, 0:n],
                                        start=(c == 0), stop=(c == DC - 1)), reads=[vbb[k2], b_ones], writes=[stb_])

                            k2s = {0: stA(0)}
                            for c in range(DC):
                                if c + 1 < DC:
                                    k2s[c + 1] = stA(c + 1)
                                stB(c, k2s[c])
                                if c >= 1:
                                    stC(c - 1, k2s[c - 1])
                            stC(DC - 1, k2s[DC - 1])
                            pinned.clear()
                            P.op("dve", _mk("tensor_scalar", out=lnm[:, 0, 0:n], in0=s1t[:, 0:n], scalar1=1.0 / D,
                                                                     scalar2=None, op0=ALU.mult), reads=[s1b], writes=[lnmb[0]])
                            P.op("dve", _mk("tensor_tensor", out=lnm[:, 1, 0:n], in0=lnm[:, 0, 0:n], in1=lnm[:, 0, 0:n],
                                                                     op=ALU.mult), reads=[lnmb[0]], writes=[lnmb[1]])
                            P.op("dve", _mk("scalar_tensor_tensor",
                                out=lnm[:, 1, 0:n], in0=s2t[:, 0:n], scalar=1.0 / D, in1=lnm[:, 1, 0:n],
                                op0=ALU.mult, op1=ALU.subtract), reads=[s2b, lnmb[1]], writes=[lnmb[1]])
                            P.op("act", _mk("activation", out=lnm[:, 2, 0:n], in_=lnm[:, 1, 0:n], func=AF.Ln,
                                                                  bias=st["epsc"][:, 0:1], scale=1.0),
                                 reads=[lnmb[1], st["epsb"]], writes=[lnmb[2]])
                            P.op("act", _mk("activation", out=s2t[:, 0:n], in_=lnm[:, 2, 0:n], func=AF.Exp,
                                                                  scale=-0.5), reads=[lnmb[2]], writes=[s2b])
                            for c in range(DC):
                                k2 = c % 2
                                P.op("dve", _mk("scalar_tensor_tensor",
                                    out=t2[:, k2, 0:n], in0=s1t[:, 0:n], scalar=-1.0 / D, in1=cvf[:, c, 0:n],
                                    op0=ALU.mult, op1=ALU.add),
                                     reads=[cvb[c], s1b], writes=[t2b[k2]])
                                P.op("dve", _mk("tensor_tensor",
                                    out=t2[:, k2, 0:n], in0=t2[:, k2, 0:n], in1=s2t[:, 0:n], op=ALU.mult),
                                     reads=[t2b[k2], s2b], writes=[t2b[k2]])
                                P.op("act", _mk("activation",
                                    out=lno[:, c, 0:n], in_=t2[:, k2, 0:n], func=AF.Silu,
                                    bias=V(f"cvlnb_{sl}", c, 1), scale=V(f"cvlng_{sl}", c, 1)),
                                     reads=[t2b[k2], b_vt], writes=[lnob[c]])
                            if DEBUG:
                                if DBG["sem"] is None:
                                    DBG["sem"] = P.new_sem("dbgsem")
                                P.barrier()
                                P.op("sp", _mk("dma_start", out=DBG["cv"].rearrange("(k p) t -> p k t", p=128)[:, :, c0:c0 + n],
                                               in_=cvf[:, :, 0:n]), reads=cvb, sem=DBG["sem"])
                                P.barrier()
                                P.op("sp", _mk("dma_start", out=DBG["ln"].rearrange("(k p) t -> p k t", p=128)[:, :, c0:c0 + n],
                                               in_=lno[:, :, 0:n]), reads=lnob, sem=DBG["sem"])
                                P.barrier()
                            proj_kd(e["w2"], lambda kc: lno[:, kc, 0:n], lnob, n,
                                    resid_evac(l, wh, xt, xtb[slot], H, n, add_c2=True))
                        else:
                            for c in range(DC):
                                rt, rb = ring_get(e["win"][c])
                                pp = [next_ps() for _ in range(3)]
                                for kc in range(DC):
                                    for blk, (pt, pb) in enumerate(pp):
                                        P.op("pe", _mk("matmul",
                                            pt[:, lo:hi], lhsT=rt[:, kc * 384 + blk * 128: kc * 384 + (blk + 1) * 128],
                                            rhs=aT[:, kc, lo:hi], start=(kc == 0), stop=(kc == DC - 1)),
                                             reads=[rb, aTb[kc]], writes=[pb])
                                (pbg, pbgb), (pcg, pcgb), (pv_, pvb) = pp
                                k2 = c % 2
                                P.op("act", _mk("activation",
                                    out=cgs[:, k2, lo:hi], in_=pcg[:, lo:hi], func=AF.Copy), reads=[pcgb], writes=[cgb[k2]])
                                if hl < H or hr < H:
                                    P.op("dve", _mk("memset", zt[:, k2, :], 0.0), writes=[ztb[k2]])
                                P.op("dve", _mk("tensor_tensor",
                                    out=zt[:, k2, lo:hi], in0=pv_[:, lo:hi], in1=cgs[:, k2, lo:hi], op=ALU.mult),
                                     reads=[pvb, cgb[k2]], writes=[ztb[k2]])
                                P.op("dve", _mk("tensor_scalar",
                                    out=acc[:, k2, 0:n], in0=zt[:, k2, 0:n], scalar1=V(f"sccv_{sl}", c, 1), scalar2=None,
                                    op0=ALU.mult), reads=[ztb[k2], b_vt], writes=[accb[k2]])
                                for k in (1, 2):
                                    P.op("dve", _mk("scalar_tensor_tensor",
                                        out=acc[:, k2, 0:n], in0=zt[:, k2, k:k + n], scalar=V(f"sccv_{sl}", k * DC + c, 1),
                                        in1=acc[:, k2, 0:n], op0=ALU.mult, op1=ALU.add),
                                         reads=[ztb[k2], accb[k2], b_vt], writes=[accb[k2]])
                                P.op("dve", _mk("tensor_tensor",
                                    out=gbt[:, c, 0:n], in0=pbg[:, H:H + n], in1=acc[:, k2, 0:n], op=ALU.mult),
                                     reads=[pbgb, accb[k2]], writes=[gbb[c]])
                            proj_kd(e["wout"], lambda kc: gbt[:, kc, 0:n], gbb, n,
                                    resid_evac(l, wh, xt, xtb[slot], H, n))
                        tail(wh, slot, c0, n, H)
                    finish_next()
                else:
                    ctx_out = ctx_live
                    tiles = make_tiles(0, True)
                    with contextlib.ExitStack() as a2:
                        def asb(name, shape, dt):
                            return a2.enter_context(nc.sbuf_tensor(f"{name}_{l}", list(shape), dt))
                        aR = asb("aR", [128, DC, TT], BF16)
                        aRb = [[Buf() for _ in range(DC)] for _ in tiles]
                        a1 = contextlib.ExitStack()
                        a1.__enter__()
                        cur[0] = a1
                        st, xts, xtb, mT, mTb = alloc_common(False)
                        load_x(0, tiles[0][1], tiles[0][2], 0)
                        for ti, (wh, c0, n, hl, hr) in enumerate(tiles):
                            slot = ti % 2
                            if ti + 1 < len(tiles):
                                load_x((ti + 1) % 2, tiles[ti + 1][1], tiles[ti + 1][2], 0)
                            norm_mod(l, wh, 0, xts[:, slot], xtb[slot], 0, n,
                                     lambda kc, c0=c0, n=n: aR[:, kc, c0:c0 + n], aRb[ti], st, sq_pool=(l > 0))
                        P.barrier()
                        a1.__exit__(None, None, None)
                        cur[0] = ls
                        QT = asb("QT", [128, TT], BF16)
                        KA = asb("KA", [128, TT], BF16)
                        KB = asb("KB", [128, TT], BF16)
                        NB = TT // 128
                        VG = asb("VG", [128, NB, 2, 128], BF16)
                        bst = asb("bst", [128, 2, 12, 128], F32)
                        Et = asb("Et", [128, 2, 12, 128], BF16)
                        pe_ = asb("pex", [128, 4, 896], BF16)
                        rd = asb("rd", [128, 4, 256], F32)
                        ast = asb("ast", [128, TT], BF16)
                        qtb = [Buf() for _ in tiles]
                        kab = [Buf() for _ in tiles]
                        kbb = [Buf() for _ in tiles]
                        vgb = [Buf() for _ in range(NB)]
                        b_bst, b_et, b_ast = Buf(), Buf(), Buf()
                        peb = [Buf() for _ in range(4)]
                        rdb = [Buf() for _ in range(4)]
                        ast_readers = []
                        bsem = P.new_sem(f"bsem{l}")
                        asem = P.new_sem(f"asem{l}")
                        b_kz = Buf()
                        P.op("dve", _mk("memset", KA[:, :], 0.0), writes=kab)
                        P.op("dve", _mk("memset", KB[:, :], 0.0), writes=kbb)
                        P.op("dve", _mk("memset", VG[:, :, :, :], 1.0), writes=vgb)
                        pei = [0]
                        rdi = [0]
                        for j in range(NP):
                            P.op("sp", _mk("dma_start",
                                out=bst[:, :, :, :], in_=nab[sl][2 * j:2 * j + 2].rearrange("h t k q -> k h t q")),
                                 writes=[b_bst], sem=bsem)
                            P.op("act", _mk("activation", out=Et[:, :, :, :], in_=bst[:, :, :, :], func=AF.Exp),
                                 reads=[b_bst], writes=[b_et])
                            rt, rb = ring_get(e["qkv"][j])
                            for ti, (wh, c0, n, hl, hr) in enumerate(tiles):
                                if wh == 0 or ctx_out:
                                    pt, pb = next_ps()
                                    for kc in range(DC):
                                        P.op("pe", _mk("matmul",
                                            pt[:, 0:n], lhsT=rt[:, kc * 384: kc * 384 + 128], rhs=aR[:, kc, c0:c0 + n],
                                            start=(kc == 0), stop=(kc == DC - 1)), reads=[rb, aRb[ti][kc]], writes=[pb])
                                    P.op("act", _mk("activation",
                                        out=QT[:, c0:c0 + n], in_=pt[:, 0:n], func=AF.Copy), reads=[pb], writes=[qtb[ti]])
                                pt, pb = next_ps()
                                for kc in range(DC):
                                    P.op("pe", _mk("matmul",
                                        pt[:, 0:n], lhsT=rt[:, kc * 384 + 128: kc * 384 + 256], rhs=aR[:, kc, c0:c0 + n],
                                        start=(kc == 0), stop=(kc == DC - 1)), reads=[rb, aRb[ti][kc]], writes=[pb])
                                P.op("act", _mk("activation",
                                    out=KA[0:64, c0:c0 + n], in_=pt[0:64, 0:n], func=AF.Copy), reads=[pb], writes=[kab[ti]])
                                P.op("dve", _mk("tensor_copy",
                                    out=KB[64:128, c0:c0 + n], in_=pt[64:128, 0:n]), reads=[pb], writes=[kbb[ti]])
                            tile_of_blk = []
                            for ti, (wh, c0, n, hl, hr) in enumerate(tiles):
                                assert c0 % 128 == 0 and n % 128 == 0
                                tile_of_blk += [ti] * (n // 128)
                            for g0 in range(0, NB, 4):
                                nb = min(4, NB - g0)
                                pt, pb = next_ps()
                                for b in range(nb):
                                    blk = g0 + b
                                    for kc in range(DC):
                                        P.op("pe", _mk("matmul",
                                            pt[:, b * 128:(b + 1) * 128], lhsT=aR[:, kc, blk * 128:(blk + 1) * 128],
                                            rhs=rt[:, kc * 384 + 256: kc * 384 + 384],
                                            start=(kc == 0), stop=(kc == DC - 1)),
                                             reads=[rb, aRb[tile_of_blk[blk]][kc]], writes=[pb])
                                pv3 = pt[:, 0:nb * 128].rearrange("p (b c) -> p b c", c=128)
                                P.op("dve", _mk("tensor_copy",
                                    out=VG[:, g0:g0 + nb, 0, 0:64], in_=pv3[:, :, 0:64]), reads=[pb], writes=vgb[g0:g0 + nb])
                                P.op("act", _mk("activation",
                                    out=VG[:, g0:g0 + nb, 1, 64:128], in_=pv3[:, :, 64:128], func=AF.Copy),
                                     reads=[pb], writes=vgb[g0:g0 + nb])
                            def qb_tile(c):
                                for ti, (wh, c0, n, hl, hr) in enumerate(tiles):
                                    if c0 <= c < c0 + n:
                                        return ti
                            ctxblks = list(range(T // 128, NB))

                            astb = {}

                            def stage1(q0, nq, kblks, head, tidx):
                                Kt, ktb = (KA, kab) if head == 0 else (KB, kbb)
                                nblk = len(kblks)
                                tot = nblk * nq
                                sp_ = [next_ps() for _ in range(-(-tot // 512))]
                                per = 512 // nq
                                qti = qb_tile(q0)
                                for i, kb in enumerate(kblks):
                                    pt, pb = sp_[i // per]
                                    o = (i % per) * nq
                                    P.op("pe", _mk("matmul",
                                        pt[:, o:o + nq], lhsT=Kt[:, kb * 128:(kb + 1) * 128], rhs=QT[:, q0:q0 + nq],
                                        start=True, stop=True),
                                         reads=[ktb[tile_of_blk[kb]], qtb[qti]], writes=[pb])
                                ps_ = pei[0] % 4
                                pei[0] += 1
                                for bi, (pt, pb) in enumerate(sp_):
                                    cw = min(512, tot - bi * 512)
                                    P.op("act", _mk("activation",
                                        out=pe_[:, ps_, bi * 512: bi * 512 + cw], in_=pt[:, 0:cw], func=AF.Exp, scale=0.125),
                                         reads=[pb], writes=[peb[ps_]])
                                if tidx is not None:
                                    nl = nblk - len(ctxblks)
                                    P.op("dve" if head == 0 else "pool", _mk("tensor_tensor",
                                        out=pe_[:, ps_, 0:nl * 128].rearrange("p (a b) -> p a b", b=128),
                                        in0=pe_[:, ps_, 0:nl * 128].rearrange("p (a b) -> p a b", b=128),
                                        in1=Et[:, head, tidx:tidx + nl, :], op=ALU.mult),
                                         reads=[peb[ps_], b_et], writes=[peb[ps_]])
                                return (q0, nq, kblks, head, ps_)

                            def stage2(state):
                                q0, nq, kblks, head, ps_ = state
                                nblk = len(kblks)
                                po, pob = next_ps()
                                for i, kb in enumerate(kblks):
                                    P.op("pe", _mk("matmul",
                                        po[:, 0:nq], lhsT=VG[:, kb, head, :], rhs=pe_[:, ps_, i * nq:(i + 1) * nq],
                                        start=(i == 0), stop=(i == nblk - 1)), reads=[vgb[kb], peb[ps_]], writes=[pob])
                                r = rdi[0] % 4
                                rdi[0] += 1
                                dlo, olo = (64, 0) if head == 0 else (0, 64)
                                ab = astb.setdefault((q0, head), Buf())
                                P.op("dve", _mk("reciprocal", out=rd[dlo:dlo + 64, r, 0:nq], in_=po[dlo:dlo + 64, 0:nq]),
                                     reads=[pob], writes=[rdb[r]])
                                P.op("dve", _mk("tensor_tensor",
                                    out=ast[olo:olo + 64, q0:q0 + nq], in0=po[olo:olo + 64, 0:nq],
                                    in1=rd[dlo:dlo + 64, r, 0:nq], op=ALU.mult), reads=[pob, rdb[r]], writes=[ab])

                            items = []
                            for qb in range(cfg.NQB):
                                kb0, nk, t0 = qb_plan(cfg, qb)
                                for head in range(2):
                                    items.append((qb * 128, 128, list(range(kb0, kb0 + nk)) + ctxblks, head, t0))
                            if ctx_out:
                                for head in range(2):
                                    items.append((T, C, ctxblks, head, None))
                            for it in items:
                                ab = astb.setdefault((it[0], it[3]), Buf())
                                ab.r = list(ast_readers)
                            DEPTH_AHEAD = 2
                            pend = []
                            for it in items:
                                pend.append(stage1(*it))
                                if len(pend) > DEPTH_AHEAD:
                                    stage2(pend.pop(0))
                            while pend:
                                stage2(pend.pop(0))
                            ncol = TT if ctx_out else T
                            ast_readers = [P.op("sp", _mk("dma_start",
                                out=ATd[j * 128:(j + 1) * 128, 0:ncol], in_=ast[:, 0:ncol]),
                                reads=list(astb.values()), sem=asem)]
                        P.barrier()
                    st, xts, xtb, mT, mTb = alloc_common(last)
                    st["hT"] = lsb("hT", [128, FC, 512], BF16)
                    st["hb"] = [Buf() for _ in range(FC)]
                    att = lsb("att", [128, 2, DC, 512], BF16)
                    attb = [[Buf() for _ in range(DC)] for _ in range(2)]
                    at_sem = [P.new_sem(f"at{l}_0"), P.new_sem(f"at{l}_1")]
                    prep_next()
                    tiles3 = make_tiles(0, ctx_out)

                    def load3(ti):
                        wh, c0, n, hl, hr = tiles3[ti]
                        load_x(ti % 2, c0, n, 0)
                        P.op("sp", _mk("dma_start",
                            out=att[:, ti % 2, :, 0:n], in_=ATd.rearrange("(k p) t -> p k t", p=128)[:, :, c0:c0 + n]),
                             writes=attb[ti % 2], sem=at_sem[ti % 2])
                    load3(0)
                    for ti, (wh, c0, n, hl, hr) in enumerate(tiles3):
                        slot = ti % 2
                        if ti + 1 < len(tiles3):
                            load3(ti + 1)
                        proj_kd(e["wo"], lambda kc: att[:, slot, kc, 0:n], attb[slot], n,
                                resid_evac(l, wh, xts[:, slot], xtb[slot], 0, n))
                        tail(wh, slot, c0, n, 0)
                    finish_next()
                P.barrier()
            src = dst

        with nc.Block() as block:
            P.lower(block)
    return nc


_CACHE = {}


def prepare_inputs(cfg, inp):
    vecs, voff = build_vecs(cfg, inp)
    nab = np.stack([build_bias_tiles(np.asarray(inp["na_rpb"][s], np.float32)) for s in range(cfg.n_na)])
    shared = {
        "vecs": vecs, "nab": nab,
        "mod_w": np.ascontiguousarray(inp["mod_w"], np.float32),
        "mlp_w1": np.ascontiguousarray(inp["mlp_w1"], np.float32),
        "mlp_w2": np.ascontiguousarray(inp["mlp_w2"], np.float32),
        "na_wqkv": np.ascontiguousarray(inp["na_wqkv"], np.float32),
        "na_wo": np.ascontiguousarray(inp["na_wo"], np.float32),
        "cv_w1": np.ascontiguousarray(inp["cv_w1"], np.float32),
        "cv_w2": np.ascontiguousarray(inp["cv_w2"], np.float32),
        "sc_win": np.ascontiguousarray(inp["sc_win"], np.float32),
        "sc_wout": np.ascontiguousarray(inp["sc_wout"], np.float32),
    }
    in_maps = []
    for b in range(np.asarray(inp["x"]).shape[0]):
        m = dict(shared)
        m["xin"] = np.ascontiguousarray(
            np.concatenate([np.asarray(inp["x"][b], np.float32).T, np.asarray(inp["ctx"][b], np.float32).T], axis=1))
        m["cc"] = np.ascontiguousarray(
            np.stack([np.asarray(inp["c"][b], np.float32), np.asarray(inp["c_ctx"], np.float32)], axis=1))
        in_maps.append(m)
    return in_maps, voff, vecs.shape[1]


def kernel(**inputs):
    B, T, D = inputs["x"].shape
    C = inputs["ctx"].shape[1]
    depth = inputs["mod_w"].shape[0]
    cfg = Cfg(D=D, T=T, C=C, depth=depth, ncores=B)
    in_maps, voff, nvec = prepare_inputs(cfg, inputs)
    key = (D, T, C, depth)
    if key not in _CACHE:
        _CACHE[key] = build_program(cfg, voff, nvec)
    nc = _CACHE[key]
    res = run_bass_kernel_spmd(nc, in_maps, core_ids=list(range(B)))
    out = np.stack([np.ascontiguousarray(r["outT"].T) for r in res.results], axis=0)
    return out.astype(np.float32)
```

```python
import numpy as np
import concourse.bass as bass
import concourse.mybir as mybir
from concourse.bass_utils import run_bass_kernel_spmd

F32 = mybir.dt.float32
BF16 = mybir.dt.bfloat16
AF = mybir.ActivationFunctionType
ALU = mybir.AluOpType

GRID_W = 64
WIN_ROWS = 8
WIN_COLS = 16
CONF_K = 31
SHORT_K = 3
EPS = 1e-6
NEG = -30000.0
SLOT = 4096
RING = 6
DEBUG = False


class Cfg:
    def __init__(self, D=1024, T=4096, C=256, depth=4, ncores=8):
        self.D, self.T, self.C, self.depth, self.ncores = D, T, C, depth, ncores
        self.DC = D // 128
        self.FF = 4 * D
        self.FC = self.FF // 128
        self.NH = D // 64
        self.NP = self.NH // 2
        self.TT = T + C
        self.rows = T // GRID_W
        self.NQB = self.rows // 2
        self.SC = min(SLOT // self.DC, D)
        self.NF = min(SLOT // D, self.FC)
        self.n_na = (depth + 2) // 3
        self.n_cv = (depth + 1) // 3
        self.n_sc = depth // 3
        assert self.SC >= 512 or True


_POS = {"matmul": ["out"], "memset": ["ap", "constant"], "iota": ["out", "pattern"]}


def _mk(meth, *a, **kw):
    names = _POS.get(meth, [])
    for nm, v in zip(names, a):
        kw[nm] = v
    assert len(a) <= len(names), (meth, len(a))
    return (meth, kw)


class Buf:
    __slots__ = ("name", "w", "r")

    def __init__(self, name=""):
        self.name = name
        self.w = None
        self.r = []


class Op:
    __slots__ = ("eng", "fn", "deps", "id", "sem", "ninc", "sig", "cnt")


class Prog:
    ENGS = ("pe", "act", "dve", "pool", "sp")

    def __init__(self, nc):
        self.nc = nc
        self.ops = []
        self.by_eng = {e: [] for e in self.ENGS}
        self.eng_sem = {e: nc.alloc_semaphore("es_" + e) for e in self.ENGS}
        self.last = {e: None for e in self.ENGS}
        self.last_sem = {}

    def new_sem(self, name):
        return self.nc.alloc_semaphore(name)

    def op(self, eng, fn, reads=(), writes=(), sem=None, ninc=1, extra=(), nobar=False):
        o = Op()
        o.eng, o.fn, o.id, o.sem, o.ninc, o.sig, o.cnt = eng, fn, len(self.ops), sem, ninc, False, None
        deps = set(extra)
        for b in reads:
            if b.w is not None:
                deps.add(b.w)
        for b in writes:
            if b.w is not None:
                deps.add(b.w)
            deps.update(b.r)
        for b in writes:
            b.w = o.id
            b.r = []
        for b in reads:
            b.r.append(o.id)
        deps.discard(o.id)
        o.deps = deps
        self.ops.append(o)
        self.by_eng[eng].append(o)
        if not nobar:
            self.last[eng] = o.id
        if sem is not None and not nobar:
            self.last_sem[id(sem)] = o.id
        return o.id

    def barrier(self):
        ids = [v for v in self.last.values() if v is not None] + list(self.last_sem.values())
        for e in self.ENGS:
            self.op(e, None, extra=ids)

    def call(self, e, fn):
        if callable(fn):
            return fn(e)
        if isinstance(fn, list):
            return [self.call(e, f) for f in fn]
        meth, kw = fn
        if kw.pop("_noncontig", False):
            with self.nc.allow_non_contiguous_dma(reason="small"):
                return getattr(e, meth)(**kw)
        return getattr(e, meth)(**kw)

    def lower(self, block):
        ops = self.ops
        for o in ops:
            for d in o.deps:
                p = ops[d]
                if p.eng == o.eng and p.eng == "pe" and p.sem is None:
                    continue
                p.sig = True
        semcnt = {}
        for o in ops:
            if o.fn is None:
                o.sig = False
            if o.sem is not None:
                o.sig = True
                c = semcnt.get(id(o.sem), 0) + 16 * o.ninc
                semcnt[id(o.sem)] = c
                o.cnt = (o.sem, c)
            elif o.sig:
                s = self.eng_sem[o.eng]
                c = semcnt.get(id(s), 0) + 1
                semcnt[id(s)] = c
                o.cnt = (s, c)
        prog = self

        def run(engname, e):
            waited = {}
            for o in prog.by_eng[engname]:
                need = {}
                for d in o.deps:
                    p = ops[d]
                    if p.cnt is None:
                        continue
                    s, c = p.cnt
                    if p.eng == engname and engname == "pe" and p.sem is None:
                        continue
                    k = id(s)
                    if c > need.get(k, (None, 0))[1]:
                        need[k] = (s, c)
                for k, (s, c) in need.items():
                    if c > waited.get(k, 0):
                        e.wait_ge(s, c)
                        waited[k] = c
                if o.fn is None:
                    continue
                res = prog.call(e, o.fn)
                if o.sem is not None:
                    insts = res if isinstance(res, (list, tuple)) else [res]
                    assert len(insts) == o.ninc, (len(insts), o.ninc)
                    for ins in insts:
                        ins.then_inc(o.sem, 16)
                elif o.sig:
                    res.then_inc(o.cnt[0], 1)

        @block.tensor
        def _(e):
            run("pe", e)

        @block.scalar
        def _(e):
            run("act", e)

        @block.vector
        def _(e):
            run("dve", e)

        @block.gpsimd
        def _(e):
            run("pool", e)

        @block.sync
        def _(e):
            run("sp", e)


def _vec(v):
    v = np.asarray(v, np.float32)
    return v.reshape(-1, 128).T


def build_vecs(cfg, inp):
    cols = []
    off = {}

    def add(name, arr):
        off[name] = sum(c.shape[1] for c in cols)
        cols.append(np.ascontiguousarray(arr, np.float32))

    for l in range(cfg.depth):
        add(f"n1g{l}", _vec(inp["norm1_g"][l]))
        add(f"n2g{l}", _vec(inp["norm2_g"][l]))
        add(f"modb{l}", _vec(inp["mod_b"][l]))
    add("fg", _vec(inp["final_g"]))
    add("ident", np.eye(128, dtype=np.float32))
    for s in range(cfg.n_cv):
        add(f"cvb1_{s}", _vec(inp["cv_b1"][s]))
        add(f"cvdwb_{s}", _vec(inp["cv_dwb"][s]))
        add(f"cvlng_{s}", _vec(inp["cv_ln_g"][s]))
        add(f"cvlnb_{s}", _vec(inp["cv_ln_b"][s]))
        add(f"cvb2_{s}", _vec(inp["cv_b2"][s]))
        add(f"cvdw_{s}", np.concatenate([_vec(inp["cv_dw"][s][k]) for k in range(CONF_K)], axis=1))
    for s in range(cfg.n_sc):
        add(f"sccv_{s}", np.concatenate([_vec(inp["sc_conv"][s][k]) for k in range(SHORT_K)], axis=1))
    return np.concatenate(cols, axis=1), off


def build_bias_tiles(rpb):
    nh = rpb.shape[0]
    kc = np.arange(GRID_W)[:, None]
    qc = np.arange(GRID_W)[None, :]
    cs = np.clip(qc - WIN_COLS // 2, 0, GRID_W - WIN_COLS)
    cmask = (kc >= cs) & (kc < cs + WIN_COLS)
    coff = np.clip(kc - qc + WIN_COLS - 1, 0, 2 * WIN_COLS - 2)

    def E(dr):
        if abs(dr) > WIN_ROWS - 1:
            return np.full((nh, GRID_W, GRID_W), NEG, np.float32)
        g = rpb[:, dr + WIN_ROWS - 1][:, coff]
        return np.where(cmask[None], g, np.float32(NEG)).astype(np.float32)

    def tile(delta, mask=None):
        t = np.full((nh, 128, 128), NEG, np.float32)
        for kh in range(2):
            for qh in range(2):
                if mask is not None and not mask(kh, qh):
                    continue
                t[:, kh * 64:(kh + 1) * 64, qh * 64:(qh + 1) * 64] = E(delta + kh - qh)
        return t

    tiles = [tile(d) for d in (-6, -4, -2, 0, 2, 4, 6)]
    tiles.append(tile(-4, lambda kh, qh: kh >= qh))
    tiles += [tile(d) for d in (-2, 0, 2)]
    tiles.append(tile(4, lambda kh, qh: kh == 0 and qh == 1))
    return np.ascontiguousarray(np.stack(tiles, axis=1))


def qb_plan(cfg, qb):
    n = cfg.NQB
    if qb == 0:
        return 0, 4, 3
    if qb == 1:
        return 0, 4, 2
    if qb == n - 2:
        return n - 4, 4, 1
    if qb == n - 1:
        return n - 4, 4, 0
    return qb - 2, 5, 7


def split_tiles(total, halo):
    k = -(-total // (512 - 2 * halo))
    base, rem = divmod(total, k)
    out, s = [], 0
    for i in range(k):
        n = base + (1 if i < rem else 0)
        out.append((s, n))
        s += n
    return out


def build_program(cfg, voff, nvec):
    D, DC, T, C, TT, FF, FC, NH, NP = cfg.D, cfg.DC, cfg.T, cfg.C, cfg.TT, cfg.FF, cfg.FC, cfg.NH, cfg.NP
    SC, NF = cfg.SC, cfg.NF
    depth = cfg.depth
    nc = bass.Bass("TRN2", target_bir_lowering=False)
    P = Prog(nc)

    def din(name, shape, dt=F32):
        return nc.dram_tensor(name, list(shape), dt, kind="ExternalInput").ap()

    xin = din("xin", [D, TT])
    ccin = din("cc", [D, 2])
    vecs_in = din("vecs", [128, nvec])
    mod_w = din("mod_w", [depth, D, 6 * D])
    mlp_w1 = din("mlp_w1", [depth, D, FF])
    mlp_w2 = din("mlp_w2", [depth, FF, D])
    na_wqkv = din("na_wqkv", [cfg.n_na, D, 3 * D])
    na_wo = din("na_wo", [cfg.n_na, D, D])
    nab = din("nab", [cfg.n_na, NH, 12, 128, 128])
    cv_w1 = din("cv_w1", [max(cfg.n_cv, 1), D, 2 * D])
    cv_w2 = din("cv_w2", [max(cfg.n_cv, 1), D, D])
    sc_win = din("sc_win", [max(cfg.n_sc, 1), D, 3 * D])
    sc_wout = din("sc_wout", [max(cfg.n_sc, 1), D, D])
    outT = nc.dram_tensor("outT", [D, T], F32, kind="ExternalOutput").ap()

    X = [nc.dram_tensor(f"xs{i}", [D, TT], F32).ap() for i in range(2)]
    ATd = nc.dram_tensor("attd", [D, TT], BF16).ap()
    DBG = {}
    if DEBUG:
        DBG["cv"] = nc.dram_tensor("dbg_cv", [D, TT], F32).ap()
        DBG["ln"] = nc.dram_tensor("dbg_ln", [D, TT], BF16).ap()
        DBG["sem"] = None

    slot_count = [0]

    def new_slots(n):
        s = slot_count[0]
        slot_count[0] += n
        return list(range(s, s + n))

    nsl_d = D // SC
    cat = []
    for l in range(depth):
        kind, sl = l % 3, l // 3
        e = {"kind": kind, "sl": sl}
        e["mod"] = new_slots(6 * D // SC)
        if kind == 0:
            e["qkv"] = new_slots(NP)
            e["wo"] = new_slots(nsl_d)
        elif kind == 1:
            e["w1"] = new_slots(DC // 2)
            e["dg"] = new_slots(DC)
            e["w2"] = new_slots(nsl_d)
        else:
            e["win"] = new_slots(DC)
            e["wout"] = new_slots(nsl_d)
        e["m1"] = new_slots(FF // SC)
        e["m2"] = new_slots(FC // NF)
        cat.append(e)
    NSLOT = slot_count[0]
    WS = nc.dram_tensor("wslots", [NSLOT, 128, SLOT], BF16).ap()

    def vcol(name, j=0, n=1):
        return (voff[name] + j, n)

    import contextlib
    es = contextlib.ExitStack()
    with es:
        def sb(name, shape, dt):
            return es.enter_context(nc.sbuf_tensor(name, list(shape), dt))

        vt = sb("vt", [128, nvec], F32)
        ones = sb("ones", [128, 128], BF16)
        ident = sb("ident", [128, 128], BF16)
        cct = sb("cct", [128, DC, 2], F32)
        sct = sb("sct", [128, DC, 2], F32)
        modv = sb("modv", [128, depth, 6 * DC, 2], F32)
        Avec = sb("Avec", [128, depth, 2, 2, DC], F32)
        c2v = sb("c2v", [128, 2, DC], F32)
        ring = sb("ring", [128, RING, SLOT], BF16)
        psum = [es.enter_context(nc.psum_tensor(f"ps{i}", [128, 512], F32)) for i in range(8)]
        psb = [Buf(f"ps{i}") for i in range(8)]
        ps_i = [0]

        pinned = set()

        def next_ps(pin=False):
            while True:
                i = ps_i[0] % 8
                ps_i[0] += 1
                if i not in pinned:
                    break
            if pin:
                pinned.add(i)
            return psum[i], psb[i]

        def V(name, j=0, n=1):
            o = voff[name] + j
            return vt[:, o:o + n]

        b_vt, b_ones, b_ident, b_sct, b_c2 = (Buf(n) for n in ("vt", "ones", "ident", "sct", "c2"))
        b_modv = [Buf(f"modv{i}") for i in range(depth)]
        b_avec = [Buf(f"avec{i}") for i in range(depth)]
        sctb = sb("sctb", [128, DC, 2], BF16)

        ring_b = [Buf(f"ring{i}") for i in range(RING)]
        ring_sem = [P.new_sem(f"rs{i}") for i in range(RING)]
        ring_q = []
        ring_state = {"emitted": 0, "used": 0}
        conv_op = {}
        slot_len = {}
        for e_ in cat:
            for k_, ln_ in (("mod", DC * SC), ("qkv", DC * 384), ("wo", DC * SC), ("w1", DC * 512), ("dg", CONF_K * 128), ("w2", DC * SC),
                            ("win", DC * 384), ("wout", DC * SC), ("m1", DC * SC), ("m2", NF * D)):
                for s_ in e_.get(k_, []):
                    slot_len[s_] = ln_

        ring_hold = {}

        def ring_emit(v):
            while ring_state["emitted"] < min(v + RING, len(ring_q)):
                x = ring_state["emitted"]
                w_ = x - RING
                if w_ >= 0 and v < w_ + ring_hold.get(w_, 0) + 1:
                    break
                sid = ring_q[x]
                if sid not in conv_op:
                    break
                pos = x % RING
                ln = slot_len[sid]
                P.op("sp", _mk("dma_start", out=ring[:, pos, 0:ln], in_=WS[sid][:, 0:ln]),
                     writes=[ring_b[pos]], sem=ring_sem[pos], extra=[conv_op[sid]])
                ring_state["emitted"] += 1

        def ring_get(sid, hold=0):
            u = ring_state["used"]
            assert ring_q[u] == sid, (u, ring_q[u], sid)
            ring_hold[u] = hold
            ring_emit(u)
            assert ring_state["emitted"] > u
            ring_state["used"] += 1
            pos = u % RING
            return ring[:, pos, :], ring_b[pos]

        c_sem = P.new_sem("csem")
        P.op("sp", lambda e: e.dma_start(out=vt[:, :], in_=vecs_in[:, :]), writes=[b_vt], sem=c_sem)
        b_cc = Buf("cc")

        def ld_cc(e):
            with nc.allow_non_contiguous_dma(reason="tiny"):
                return e.dma_start(out=cct[:, :, :], in_=ccin.rearrange("(k p) t -> p k t", p=128))
        P.op("sp", ld_cc, writes=[b_cc], sem=P.new_sem("csem2"))
        P.op("dve", lambda e: e.memset(ones[:, :], 1.0), writes=[b_ones])
        P.op("dve", _mk("tensor_copy", out=ident[:, :], in_=V("ident", 0, 128)), reads=[b_vt], writes=[b_ident])

        def conv_group(name, parts):
            sem = P.new_sem("cg_" + name)

            def fn(e, parts=parts):
                return [e.dma_start(out=d, in_=s) for d, s in parts]
            return P.op("pool", fn, sem=sem, ninc=len(parts), nobar=True)

        def parts_kd(slots, src2d):
            return [(WS[sid][:, 0:DC * SC].rearrange("p (k n) -> p k n", k=DC),
                     src2d[:, i * SC:(i + 1) * SC].rearrange("(k p) n -> p k n", p=128)) for i, sid in enumerate(slots)]

        def parts_blocks(slots, src2d, blocks):
            r = []
            for sid, blks in zip(slots, blocks):
                nb = len(blks)
                dst = WS[sid][:, 0:DC * nb * 128].rearrange("p (k n) -> p k n", k=DC)
                for bi, c0 in enumerate(blks):
                    r.append((dst[:, :, bi * 128:(bi + 1) * 128],
                              src2d[:, c0:c0 + 128].rearrange("(k p) n -> p k n", p=128)))
            return r

        def parts_w2(slots, src2d):
            return [(WS[sid][:, 0:NF * D].rearrange("p (f n) -> p f n", f=NF),
                     src2d[i * NF * 128:(i + 1) * NF * 128, :].rearrange("(f p) n -> p f n", p=128))
                    for i, sid in enumerate(slots)]

        def reg(slots, opid):
            for s in slots:
                conv_op[s] = opid

        dgs_sem = [P.new_sem("dg0"), P.new_sem("dg1")]

        def emit_convs(l, part="all"):
            e = cat[l]
            kind, sl = e["kind"], e["sl"]
            if part in ("all", "mod"):
                reg(e["mod"], conv_group(f"mod{l}", parts_kd(e["mod"], mod_w[l])))
            if part == "mod":
                return
            if kind == 0:
                blocks = [[j * 128, D + j * 128, 2 * D + j * 128] for j in range(NP)]
                reg(e["qkv"], conv_group(f"qkv{l}", parts_blocks(e["qkv"], na_wqkv[sl], blocks)))
                reg(e["wo"], conv_group(f"wo{l}", parts_kd(e["wo"], na_wo[sl])))
            elif kind == 1:
                blocks = [[2 * s * 128, D + 2 * s * 128, (2 * s + 1) * 128, D + (2 * s + 1) * 128] for s in range(DC // 2)]
                reg(e["w1"], conv_group(f"cw1{l}", parts_blocks(e["w1"], cv_w1[sl], blocks)))
                reg(e["w2"], conv_group(f"cw2{l}", parts_kd(e["w2"], cv_w2[sl])))
            else:
                blocks = [[c * 128, D + c * 128, 2 * D + c * 128] for c in range(DC)]
                reg(e["win"], conv_group(f"swi{l}", parts_blocks(e["win"], sc_win[sl], blocks)))
                reg(e["wout"], conv_group(f"swo{l}", parts_kd(e["wout"], sc_wout[sl])))
            reg(e["m1"], conv_group(f"m1{l}", parts_kd(e["m1"], mlp_w1[l])))
            reg(e["m2"], conv_group(f"m2{l}", parts_w2(e["m2"], mlp_w2[l])))

        def emit_diag(l, alloc):
            e = cat[l]
            kind, sl = e["kind"], e["sl"]
            if kind == 1:
                dgs = alloc(f"dgs{l}", [128, 2, CONF_K * 128], BF16)
                dgs_b = [Buf("dgs0"), Buf("dgs1")]
                for c in range(DC):
                    s = c % 2
                    for k in range(CONF_K):
                        P.op("dve", _mk("tensor_scalar",
                            out=dgs[:, s, k * 128:(k + 1) * 128], in0=ident[:, :],
                            scalar1=V(f"cvdw_{sl}", k * DC + c, 1), scalar2=None, op0=ALU.mult),
                             reads=[b_ident, b_vt], writes=[dgs_b[s]])
                    sid = e["dg"][c]
                    conv_op[sid] = P.op("sp", _mk("dma_start",
                        out=WS[sid][:, 0:CONF_K * 128], in_=dgs[:, s, :]), reads=[dgs_b[s]], sem=dgs_sem[s])

        def emit_mod_slots(l, sis):
            e = cat[l]
            bps = SC // 128
            for si in sis:
                sid = e["mod"][si]
                rt, rb = ring_get(sid)
                pt, pb = next_ps()
                pv = pt[:, 0:bps * 2].rearrange("p (c t) -> p c t", t=2)
                for j in range(bps):
                    for kc in range(DC):
                        P.op("pe", _mk("matmul",
                            pv[:, j, :], lhsT=rt[:, kc * SC + j * 128: kc * SC + (j + 1) * 128], rhs=sctb[:, kc, :],
                            start=(kc == 0), stop=(kc == DC - 1)), reads=[rb, b_sct], writes=[pb])
                for wh in range(2):
                    P.op("dve", _mk("tensor_tensor",
                        out=modv[:, l, si * bps:(si + 1) * bps, wh], in0=pv[:, :, wh],
                        in1=V(f"modb{l}", si * bps, bps), op=ALU.add),
                         reads=[pb, b_vt], writes=[b_modv[l]])

        def mod_chunks(l, ntiles):
            ns = len(cat[l]["mod"])
            per = max(2, -(-ns // ntiles))
            nch = -(-ns // per)
            off = ntiles - nch
            return [list(range((ti - off) * per, min(ns, (ti - off + 1) * per))) if ti >= off else []
                    for ti in range(ntiles)]

        def emit_mod(l):
            ring_q.extend(cat[l]["mod"])
            emit_mod_slots(l, range(len(cat[l]["mod"])))
            emit_mod_final(l)

        def emit_mod_final(l):
            for wh in range(2):
                for nm, (gname, mi) in enumerate((("n1g", 1), ("n2g", 4))):
                    P.op("dve", _mk("scalar_tensor_tensor",
                        out=Avec[:, l, wh, nm, :], in0=modv[:, l, mi * DC:(mi + 1) * DC, wh], scalar=1.0,
                        in1=V(f"{gname}{l}", 0, DC), op0=ALU.add, op1=ALU.mult),
                         reads=[b_modv[l], b_vt], writes=[b_avec[l]])
            if cat[l]["kind"] == 1:
                sl = cat[l]["sl"]
                for wh in range(2):
                    P.op("dve", _mk("tensor_tensor",
                        out=c2v[:, wh, :], in0=modv[:, l, 2 * DC:3 * DC, wh], in1=V(f"cvb2_{sl}", 0, DC),
                        op=ALU.mult), reads=[b_modv[l], b_vt], writes=[b_c2])

        P.op("act", lambda e: e.activation(out=sct[:, :, :], in_=cct[:, :, :], func=AF.Silu),
             reads=[b_cc], writes=[b_sct])
        P.op("act", lambda e: e.activation(out=sctb[:, :, :], in_=sct[:, :, :], func=AF.Copy),
             reads=[b_sct], writes=[b_sct])
        emit_convs(0)
        emit_diag(0, sb)
        emit_mod(0)

        def mod_col(l, m, kc, wh):
            return modv[:, l, m * DC + kc, wh:wh + 1]

        def norm_mod(l, wh, nm, xt, xb, lo, hi, out_fn, out_bufs, st, sq_pool=False):
            s_ = norm_stats(xt, xb, lo, hi, st, sq_pool)
            norm_apply(l, wh, nm, xt, xb, lo, hi, out_fn, out_bufs, st, s_)

        def norm_stats(xt, xb, lo, hi, st, sq_pool=False):
            w = hi - lo
            pt, pb = next_ps()
            for kc in range(DC):
                s = st["sqi"] % 4
                st["sqi"] += 1
                if sq_pool and kc % 2 == 1:
                    P.op("pool", _mk("tensor_tensor", out=st["sq"][:, s, 0:w], in0=xt[:, kc, lo:hi],
                                     in1=xt[:, kc, lo:hi], op=ALU.mult), reads=[xb[kc]], writes=[st["sqb"][s]])
                else:
                    P.op("act", _mk("activation", out=st["sq"][:, s, 0:w], in_=xt[:, kc, lo:hi],
                                                              func=AF.Square),
                         reads=[xb[kc]], writes=[st["sqb"][s]])
                P.op("pe", _mk("matmul", pt[:, 0:w], lhsT=ones[:, :], rhs=st["sq"][:, s, 0:w],
                                                          start=(kc == 0), stop=(kc == DC - 1)),
                     reads=[st["sqb"][s], b_ones], writes=[pb])
            r = st["rsi"] % 2
            st["rsi"] += 1
            rs, rsb = st["rs"], st["rsb"][r]
            P.op("act", _mk("activation", out=rs[:, r, 0:w], in_=pt[:, 0:w], func=AF.Ln, scale=1.0 / D,
                                                bias=st["epsc"][:, 0:1]), reads=[pb, st["epsb"]], writes=[rsb])
            P.op("act", _mk("activation", out=pt[:, 0:w], in_=rs[:, r, 0:w], func=AF.Exp, scale=-0.5),
                 reads=[rsb], writes=[pb])
            return pt, pb

        def norm_apply(l, wh, nm, xt, xb, lo, hi, out_fn, out_bufs, st, s_):
            pt, pb = s_
            w = hi - lo
            for kc in range(DC):
                s = st["tmi"] % 2
                st["tmi"] += 1
                if nm is None:
                    P.op("dve", _mk("scalar_tensor_tensor",
                        out=out_fn(kc), in0=xt[:, kc, lo:hi], scalar=V("fg", kc, 1), in1=pt[:, 0:w],
                        op0=ALU.mult, op1=ALU.mult), reads=[xb[kc], pb, b_vt], writes=[out_bufs[kc]])
                    continue
                P.op("dve", _mk("scalar_tensor_tensor",
                    out=st["tm"][:, s, 0:w], in0=xt[:, kc, lo:hi], scalar=Avec[:, l, wh, nm, kc:kc + 1],
                    in1=pt[:, 0:w], op0=ALU.mult, op1=ALU.mult),
                     reads=[xb[kc], pb, b_avec[l]], writes=[st["tmb"][s]])
                P.op("act", _mk("activation",
                    out=out_fn(kc), in_=st["tm"][:, s, 0:w], func=AF.Identity,
                    bias=mod_col(l, 0 if nm == 0 else 3, kc, wh), scale=1.0),
                     reads=[st["tmb"][s], b_modv[l]], writes=[out_bufs[kc]])

        def proj_kd(slots, rhs_fn, rhs_bufs, w, evac):
            bps = SC // 128
            for si, sid in enumerate(slots):
                rt, rb = ring_get(sid)
                grp = [next_ps() for _ in range(bps)]
                for kc in range(DC):
                    for j in range(bps):
                        pt, pb = grp[j]
                        P.op("pe", _mk("matmul",
                            pt[:, 0:w], lhsT=rt[:, kc * SC + j * 128: kc * SC + (j + 1) * 128], rhs=rhs_fn(kc),
                            start=(kc == 0), stop=(kc == DC - 1)), reads=[rb, rhs_bufs[kc]], writes=[pb])
                for j in range(bps):
                    evac(si * bps + j, grp[j][0], grp[j][1])

        def mlp(l, wh, mt, mtb, w, xt, xb, lo, st):
            hT, hb = st["hT"], st["hb"]
            e = cat[l]
            bps = SC // 128
            for si, sid in enumerate(e["m1"]):
                rt, rb = ring_get(sid)
                grp = [next_ps() for _ in range(bps)]
                for kc in range(DC):
                    for j in range(bps):
                        pt, pb = grp[j]
                        P.op("pe", _mk("matmul",
                            pt[:, 0:w], lhsT=rt[:, kc * SC + j * 128: kc * SC + (j + 1) * 128], rhs=mt[:, kc, 0:w],
                            start=(kc == 0), stop=(kc == DC - 1)), reads=[rb, mtb[kc]], writes=[pb])
                for j in range(bps):
                    fc = si * bps + j
                    pt, pb = grp[j]
                    s = st["rli"] % 2
                    st["rli"] += 1
                    P.op("act", _mk("activation", out=st["rl"][:, s, 0:w], in_=pt[:, 0:w],
                                                                    func=AF.Relu),
                         reads=[pb], writes=[st["rlb"][s]])
                    P.op("dve", _mk("tensor_tensor",
                        out=hT[:, fc, 0:w], in0=st["rl"][:, s, 0:w], in1=st["rl"][:, s, 0:w], op=ALU.mult),
                         reads=[st["rlb"][s]], writes=[hb[fc]])
            accs = [next_ps() for _ in range(DC)]
            for si, sid in enumerate(e["m2"]):
                rt, rb = ring_get(sid)
                for f in range(NF):
                    fc = si * NF + f
                    for dc in range(DC):
                        pt, pb = accs[dc]
                        P.op("pe", _mk("matmul",
                            pt[:, 0:w], lhsT=rt[:, f * D + dc * 128: f * D + (dc + 1) * 128], rhs=hT[:, fc, 0:w],
                            start=(fc == 0), stop=(fc == FC - 1)), reads=[rb, hb[fc]], writes=[pb])
            for dc in range(DC):
                pt, pb = accs[dc]
                P.op("dve", _mk("scalar_tensor_tensor",
                    out=xt[:, dc, lo:lo + w], in0=pt[:, 0:w], scalar=mod_col(l, 5, dc, wh), in1=xt[:, dc, lo:lo + w],
                    op0=ALU.mult, op1=ALU.add), reads=[pb, b_modv[l], xb[dc]], writes=[xb[dc]])

        def resid_evac(l, wh, xt, xb, lo, w, add_c2=False):
            def ev(oc, pt, pb):
                P.op("dve", _mk("scalar_tensor_tensor",
                    out=xt[:, oc, lo:lo + w], in0=pt[:, 0:w], scalar=mod_col(l, 2, oc, wh), in1=xt[:, oc, lo:lo + w],
                    op0=ALU.mult, op1=ALU.add), reads=[pb, b_modv[l], xb[oc]], writes=[xb[oc]])
                if add_c2:
                    P.op("dve", _mk("tensor_scalar",
                        out=xt[:, oc, lo:lo + w], in0=xt[:, oc, lo:lo + w], scalar1=c2v[:, wh, oc:oc + 1], scalar2=None,
                        op0=ALU.add), reads=[xb[oc], b_c2], writes=[xb[oc]])
            return ev

        def make_tiles(halo, with_ctx):
            tl = []
            for (s, n) in split_tiles(T, halo):
                tl.append((0, s, n, min(halo, s), min(halo, T - s - n)))
            if with_ctx:
                for (s, n) in split_tiles(C, halo):
                    tl.append((1, T + s, n, min(halo, s), min(halo, C - s - n)))
            return tl

        last_attn_ = max(i for i in range(depth) if i % 3 == 0)

        def ring_seq_main(l):
            e = cat[l]
            kind = e["kind"]
            live = l < last_attn_
            seq = []
            tl_ = make_tiles({0: 0, 1: CONF_K // 2, 2: SHORT_K // 2}[kind], live)
            mch = mod_chunks(l + 1, len(tl_)) if l + 1 < depth else [[] for _ in tl_]
            if kind != 0:
                for ti_, _t in enumerate(tl_):
                    if kind == 1:
                        seq.append(e["w1"][0])
                        for c in range(DC):
                            if c + 1 < DC and (c + 1) % 2 == 0:
                                seq.append(e["w1"][(c + 1) // 2])
                            seq.append(e["dg"][c])
                        seq.extend(e["w2"])
                    else:
                        seq.extend(e["win"])
                        seq.extend(e["wout"])
                    seq.extend(cat[l + 1]["mod"][si] for si in mch[ti_])
                    seq.extend(e["m1"])
                    seq.extend(e["m2"])
            else:
                seq.extend(e["qkv"])
                for ti_, _t in enumerate(tl_):
                    seq.extend(e["wo"])
                    seq.extend(cat[l + 1]["mod"][si] for si in mch[ti_])
                    seq.extend(e["m1"])
                    seq.extend(e["m2"])
            return seq

        ring_q.extend(ring_seq_main(0))
        x_sem = [P.new_sem("xs0"), P.new_sem("xs1")]
        st_sem = [P.new_sem("st0"), P.new_sem("st1")]

        src = xin
        for l in range(depth):
            e = cat[l]
            kind, sl = e["kind"], e["sl"]
            last = (l == depth - 1)
            last_attn = max(i for i in range(depth) if i % 3 == 0)
            ctx_live = l < last_attn
            dst = X[l % 2]
            with contextlib.ExitStack() as ls:
                cur = [ls]
                uniq = [0]

                def lsb(name, shape, dt):
                    uniq[0] += 1
                    return cur[0].enter_context(nc.sbuf_tensor(f"{name}_{l}_{uniq[0]}", list(shape), dt))

                halo = {0: 0, 1: CONF_K // 2, 2: SHORT_K // 2}[kind]

                if l + 1 < depth and kind != 0:
                    emit_convs(l + 1, "all")

                def prep_next():
                    if l + 1 < depth:
                        if kind == 0:
                            emit_convs(l + 1, "all")
                        emit_diag(l + 1, lsb)

                def finish_next():
                    if l + 1 < depth:
                        emit_mod_final(l + 1)
                        ring_q.extend(ring_seq_main(l + 1))
                        ring_emit(ring_state["used"])

                def alloc_common(with_fo=False):
                    st_ = {"sqi": 0, "rsi": 0, "tmi": 0, "rli": 0}
                    st_["sq"] = lsb("sq", [128, 4, 512], BF16)
                    st_["sqb"] = [Buf() for _ in range(4)]
                    st_["rs"] = lsb("rs", [128, 2, 512], F32)
                    st_["rsb"] = [Buf(), Buf()]
                    st_["tm"] = lsb("tm", [128, 2, 512], F32)
                    st_["tmb"] = [Buf(), Buf()]
                    st_["rl"] = lsb("rl", [128, 2, 512], F32)
                    st_["rlb"] = [Buf(), Buf()]
                    st_["epsc"] = lsb("epsc", [128, 1], F32)
                    st_["epsb"] = Buf()
                    P.op("dve", _mk("memset", st_["epsc"][:, :], EPS), writes=[st_["epsb"]])
                    xts_ = lsb("xt", [128, 2, DC, 512], F32)
                    xtb_ = [[Buf() for _ in range(DC)] for _ in range(2)]
                    mT_ = lsb("mT", [128, DC, 512], BF16)
                    mTb_ = [Buf() for _ in range(DC)]
                    if with_fo:
                        st_["fo"] = lsb("fo", [128, DC, 512], F32)
                        st_["fob"] = [Buf() for _ in range(DC)]
                        st_["fosem"] = P.new_sem(f"fosem{uniq[0]}")
                    return st_, xts_, xtb_, mT_, mTb_

                if kind != 0:
                    st, xts, xtb, mT, mTb = alloc_common(last)

                def load_x(slot, col0, ncols, lo):
                    P.op("sp", _mk("dma_start",
                        out=xts[:, slot, :, lo:lo + ncols],
                        in_=src.rearrange("(k p) t -> p k t", p=128)[:, :, col0:col0 + ncols]),
                         writes=xtb[slot], sem=x_sem[slot])

                def store_x(slot, col0, ncols, lo, dram):
                    P.op("sp", _mk("dma_start",
                        out=dram.rearrange("(k p) t -> p k t", p=128)[:, :, col0:col0 + ncols],
                        in_=xts[:, slot, :, lo:lo + ncols]), reads=xtb[slot], sem=st_sem[slot])

                def tail(wh, slot, col0, n, lo, ti, ntl):
                    xt = xts[:, slot]
                    norm_mod(l, wh, 1, xt, xtb[slot], lo, lo + n, lambda kc: mT[:, kc, 0:n], mTb, st)
                    if l + 1 < depth:
                        emit_mod_slots(l + 1, mod_chunks(l + 1, ntl)[ti])
                    mlp(l, wh, mT, mTb, n, xt, xtb[slot], lo, st)
                    if last:
                        fo = st["fo"]
                        norm_mod(l, wh, None, xt, xtb[slot], lo, lo + n, lambda kc: fo[:, kc, 0:n], st["fob"], st)
                        P.op("sp", _mk("dma_start",
                            out=outT.rearrange("(k p) t -> p k t", p=128)[:, :, col0:col0 + n], in_=fo[:, :, 0:n]),
                             reads=st["fob"], sem=st["fosem"])
                    else:
                        store_x(slot, col0, n, lo, dst)

                if kind != 0:
                    st["hT"] = lsb("hT", [128, FC, 512], BF16)
                    st["hb"] = [Buf() for _ in range(FC)]
                    tiles = make_tiles(halo, ctx_live)
                    aT = lsb("aT", [128, DC, 512], BF16)
                    aTb = [Buf() for _ in range(DC)]
                    if kind == 1:
                        sg = lsb("sg", [128, 2, 512], F32)
                        sgb = [Buf(), Buf()]
                        ub = lsb("ub", [128, 2, 512], BF16)
                        ubb = [Buf(), Buf()]
                        cvf = lsb("cvf", [128, DC, 512], F32)
                        cvb = [Buf() for _ in range(DC)]
                        vb = lsb("vb", [128, 2, 2, 512], BF16)
                        vbb = [Buf(), Buf()]
                        lnm = lsb("lnm", [128, 4, 512], F32)
                        lnmb = [Buf() for _ in range(4)]
                        t2 = lsb("t2", [128, 2, 512], F32)
                        t2b = [Buf(), Buf()]
                        lno = lsb("lno", [128, DC, 512], BF16)
                        lnob = [Buf() for _ in range(DC)]
                    else:
                        cgs = lsb("cgs", [128, 2, 512], F32)
                        cgb = [Buf(), Buf()]
                        zt = lsb("zt", [128, 2, 512], F32)
                        ztb = [Buf(), Buf()]
                        acc = lsb("acc", [128, 2, 512], F32)
                        accb = [Buf(), Buf()]
                        gbt = lsb("gbt", [128, DC, 512], BF16)
                        gbb = [Buf() for _ in range(DC)]
                    prep_next()
                    H = halo

                    def issue_load(ti):
                        wh, c0, n, hl, hr = tiles[ti]
                        load_x(ti % 2, c0 - hl, n + hl + hr, H - hl)
                    issue_load(0)
                    ci = [0]
                    for ti, (wh, c0, n, hl, hr) in enumerate(tiles):
                        slot = ti % 2
                        if ti + 1 < len(tiles):
                            issue_load(ti + 1)
                        xt = xts[:, slot]
                        lo, hi = H - hl, H + n + hr
                        w = hi - lo
                        norm_mod(l, wh, 0, xt, xtb[slot], lo, hi, lambda kc: aT[:, kc, lo:hi], aTb, st)
                        if kind == 1:
                            s1t, s1b = next_ps(pin=True)
                            s2t, s2b = next_ps(pin=True)
                            w1state = {}

                            def stA(c):
                                s, cc = divmod(c, 2)
                                if cc == 0:
                                    w1state["rt"], w1state["rb"] = ring_get(e["w1"][s], hold=2)
                                rt, rb = w1state["rt"], w1state["rb"]
                                pa, pab = next_ps()
                                pg, pgb = next_ps()
                                for kc in range(DC):
                                    for (pt, pb, blk) in ((pa, pab, 2 * cc), (pg, pgb, 2 * cc + 1)):
                                        P.op("pe", _mk("matmul",
                                            pt[:, lo:hi], lhsT=rt[:, kc * 512 + blk * 128: kc * 512 + (blk + 1) * 128],
                                            rhs=aT[:, kc, lo:hi], start=(kc == 0), stop=(kc == DC - 1)),
                                             reads=[rb, aTb[kc]], writes=[pb])
                                k2 = ci[0] % 2
                                ci[0] += 1
                                P.op("act", _mk("activation",
                                    out=sg[:, k2, lo:hi], in_=pg[:, lo:hi], func=AF.Sigmoid,
                                    bias=V(f"cvb1_{sl}", DC + c, 1), scale=1.0), reads=[pgb, b_vt], writes=[sgb[k2]])
                                if hl < H or hr < H:
                                    P.op("dve", _mk("memset", ub[:, k2, :], 0.0), writes=[ubb[k2]])
                                P.op("dve", _mk("scalar_tensor_tensor",
                                    out=ub[:, k2, lo:hi], in0=pa[:, lo:hi], scalar=V(f"cvb1_{sl}", c, 1),
                                    in1=sg[:, k2, lo:hi], op0=ALU.add, op1=ALU.mult),
                                     reads=[pab, sgb[k2], b_vt], writes=[ubb[k2]])
                                return k2

                            def stB(c, k2):
                                dt_, db_ = ring_get(e["dg"][c])
                                pc, pcb = next_ps()
                                for k in range(CONF_K):
                                    P.op("pe", _mk("matmul",
                                        pc[:, 0:n], lhsT=dt_[:, k * 128:(k + 1) * 128], rhs=ub[:, k2, k:k + n],
                                        start=(k == 0), stop=(k == CONF_K - 1)), reads=[db_, ubb[k2]], writes=[pcb])
                                P.op("act", _mk("activation",
                                    out=cvf[:, c, 0:n], in_=pc[:, 0:n], func=AF.Identity,
                                    bias=V(f"cvdwb_{sl}", c, 1), scale=1.0), reads=[pcb, b_vt], writes=[cvb[c]])
                                P.op("act", _mk("activation",
                                    out=vb[:, k2, 0, 0:n], in_=pc[:, 0:n], func=AF.Identity,
                                    bias=V(f"cvdwb_{sl}", c, 1), scale=1.0), reads=[pcb, b_vt], writes=[vbb[k2]])
                                P.op("act", _mk("activation",
                                    out=vb[:, k2, 1, 0:n], in_=pc[:, 0:n], func=AF.Square,
                                    bias=V(f"cvdwb_{sl}", c, 1), scale=1.0), reads=[pcb, b_vt, vbb[k2]], writes=[vbb[k2]])

                            def stC(c, k2):
                                for q, (stt_, stb_) in enumerate(((s1t, s1b), (s2t, s2b))):
                                    P.op("pe", _mk("matmul",
                                        stt_[:, 0:n], lhsT=ones[:, :], rhs=vb[:, k2, q, 0:n],
                                        start=(c == 0), stop=(c == DC - 1)), reads=[vbb[k2], b_ones], writes=[stb_])

                            k2s = {0: stA(0)}
                            for c in range(DC):
                                if c + 1 < DC:
                                    k2s[c + 1] = stA(c + 1)
                                stB(c, k2s[c])
                                if c >= 1:
                                    stC(c - 1, k2s[c - 1])
                            stC(DC - 1, k2s[DC - 1])
                            pinned.clear()
                            P.op("dve", _mk("tensor_scalar", out=lnm[:, 0, 0:n], in0=s1t[:, 0:n], scalar1=1.0 / D,
                                                                     scalar2=None, op0=ALU.mult), reads=[s1b], writes=[lnmb[0]])
                            P.op("dve", _mk("tensor_tensor", out=lnm[:, 1, 0:n], in0=lnm[:, 0, 0:n], in1=lnm[:, 0, 0:n],
                                                                     op=ALU.mult), reads=[lnmb[0]], writes=[lnmb[1]])
                            P.op("dve", _mk("scalar_tensor_tensor",
                                out=lnm[:, 1, 0:n], in0=s2t[:, 0:n], scalar=1.0 / D, in1=lnm[:, 1, 0:n],
                                op0=ALU.mult, op1=ALU.subtract), reads=[s2b, lnmb[1]], writes=[lnmb[1]])
                            P.op("act", _mk("activation", out=lnm[:, 2, 0:n], in_=lnm[:, 1, 0:n], func=AF.Ln,
                                                                  bias=st["epsc"][:, 0:1], scale=1.0),
                                 reads=[lnmb[1], st["epsb"]], writes=[lnmb[2]])
                            P.op("act", _mk("activation", out=s2t[:, 0:n], in_=lnm[:, 2, 0:n], func=AF.Exp,
                                                                  scale=-0.5), reads=[lnmb[2]], writes=[s2b])
                            for c in range(DC):
                                k2 = c % 2
                                P.op("dve", _mk("scalar_tensor_tensor",
                                    out=t2[:, k2, 0:n], in0=s1t[:, 0:n], scalar=-1.0 / D, in1=cvf[:, c, 0:n],
                                    op0=ALU.mult, op1=ALU.add),
                                     reads=[cvb[c], s1b], writes=[t2b[k2]])
                                P.op("dve", _mk("tensor_tensor",
                                    out=t2[:, k2, 0:n], in0=t2[:, k2, 0:n], in1=s2t[:, 0:n], op=ALU.mult),
                                     reads=[t2b[k2], s2b], writes=[t2b[k2]])
                                P.op("act", _mk("activation",
                                    out=lno[:, c, 0:n], in_=t2[:, k2, 0:n], func=AF.Silu,
                                    bias=V(f"cvlnb_{sl}", c, 1), scale=V(f"cvlng_{sl}", c, 1)),
                                     reads=[t2b[k2], b_vt], writes=[lnob[c]])
                            if DEBUG:
                                if DBG["sem"] is None:
                                    DBG["sem"] = P.new_sem("dbgsem")
                                P.barrier()
                                P.op("sp", _mk("dma_start", out=DBG["cv"].rearrange("(k p) t -> p k t", p=128)[:, :, c0:c0 + n],
                                               in_=cvf[:, :, 0:n]), reads=cvb, sem=DBG["sem"])
                                P.barrier()
                                P.op("sp", _mk("dma_start", out=DBG["ln"].rearrange("(k p) t -> p k t", p=128)[:, :, c0:c0 + n],
                                               in_=lno[:, :, 0:n]), reads=lnob, sem=DBG["sem"])
                                P.barrier()
                            proj_kd(e["w2"], lambda kc: lno[:, kc, 0:n], lnob, n,
                                    resid_evac(l, wh, xt, xtb[slot], H, n, add_c2=True))
                        else:
                            for c in range(DC):
                                rt, rb = ring_get(e["win"][c])
                                pp = [next_ps() for _ in range(3)]
                                for kc in range(DC):
                                    for blk, (pt, pb) in enumerate(pp):
                                        P.op("pe", _mk("matmul",
                                            pt[:, lo:hi], lhsT=rt[:, kc * 384 + blk * 128: kc * 384 + (blk + 1) * 128],
                                            rhs=aT[:, kc, lo:hi], start=(kc == 0), stop=(kc == DC - 1)),
                                             reads=[rb, aTb[kc]], writes=[pb])
                                (pbg, pbgb), (pcg, pcgb), (pv_, pvb) = pp
                                k2 = c % 2
                                P.op("act", _mk("activation",
                                    out=cgs[:, k2, lo:hi], in_=pcg[:, lo:hi], func=AF.Copy), reads=[pcgb], writes=[cgb[k2]])
                                if hl < H or hr < H:
                                    P.op("dve", _mk("memset", zt[:, k2, :], 0.0), writes=[ztb[k2]])
                                P.op("dve", _mk("tensor_tensor",
                                    out=zt[:, k2, lo:hi], in0=pv_[:, lo:hi], in1=cgs[:, k2, lo:hi], op=ALU.mult),
                                     reads=[pvb, cgb[k2]], writes=[ztb[k2]])
                                P.op("dve", _mk("tensor_scalar",
                                    out=acc[:, k2, 0:n], in0=zt[:, k2, 0:n], scalar1=V(f"sccv_{sl}", c, 1), scalar2=None,
                                    op0=ALU.mult), reads=[ztb[k2], b_vt], writes=[accb[k2]])
                                for k in (1, 2):
                                    P.op("dve", _mk("scalar_tensor_tensor",
                                        out=acc[:, k2, 0:n], in0=zt[:, k2, k:k + n], scalar=V(f"sccv_{sl}", k * DC + c, 1),
                                        in1=acc[:, k2, 0:n], op0=ALU.mult, op1=ALU.add),
                                         reads=[ztb[k2], accb[k2], b_vt], writes=[accb[k2]])
                                P.op("dve", _mk("tensor_tensor",
                                    out=gbt[:, c, 0:n], in0=pbg[:, H:H + n], in1=acc[:, k2, 0:n], op=ALU.mult),
                                     reads=[pbgb, accb[k2]], writes=[gbb[c]])
                            proj_kd(e["wout"], lambda kc: gbt[:, kc, 0:n], gbb, n,
                                    resid_evac(l, wh, xt, xtb[slot], H, n))
                        tail(wh, slot, c0, n, H, ti, len(tiles))
                    finish_next()
                else:
                    ctx_out = ctx_live
                    tiles = make_tiles(0, True)
                    with contextlib.ExitStack() as a2:
                        def asb(name, shape, dt):
                            return a2.enter_context(nc.sbuf_tensor(f"{name}_{l}", list(shape), dt))
                        aR = asb("aR", [128, DC, TT], BF16)
                        aRb = [[Buf() for _ in range(DC)] for _ in tiles]
                        a1 = contextlib.ExitStack()
                        a1.__enter__()
                        cur[0] = a1
                        st, xts, xtb, mT, mTb = alloc_common(False)
                        load_x(0, tiles[0][1], tiles[0][2], 0)
                        for ti, (wh, c0, n, hl, hr) in enumerate(tiles):
                            slot = ti % 2
                            if ti + 1 < len(tiles):
                                load_x((ti + 1) % 2, tiles[ti + 1][1], tiles[ti + 1][2], 0)
                            norm_mod(l, wh, 0, xts[:, slot], xtb[slot], 0, n,
                                     lambda kc, c0=c0, n=n: aR[:, kc, c0:c0 + n], aRb[ti], st, sq_pool=(l > 0))
                        P.barrier()
                        a1.__exit__(None, None, None)
                        cur[0] = ls
                        QT = asb("QT", [128, TT], BF16)
                        KA = asb("KA", [128, TT], BF16)
                        KB = asb("KB", [128, TT], BF16)
                        NB = TT // 128
                        VG = asb("VG", [128, NB, 2, 128], BF16)
                        bst = asb("bst", [128, 2, 12, 128], F32)
                        Et = asb("Et", [128, 2, 12, 128], BF16)
                        pe_ = asb("pex", [128, 4, 896], BF16)
                        rd = asb("rd", [128, 4, 256], F32)
                        ast = asb("ast", [128, TT], BF16)
                        qtb = [Buf() for _ in tiles]
                        kab = [Buf() for _ in tiles]
                        kbb = [Buf() for _ in tiles]
                        vgb = [Buf() for _ in range(NB)]
                        b_bst, b_et, b_ast = Buf(), Buf(), Buf()
                        peb = [Buf() for _ in range(4)]
                        rdb = [Buf() for _ in range(4)]
                        ast_readers = []
                        bsem = P.new_sem(f"bsem{l}")
                        asem = P.new_sem(f"asem{l}")
                        b_kz = Buf()
                        P.op("dve", _mk("memset", KA[:, :], 0.0), writes=kab)
                        P.op("dve", _mk("memset", KB[:, :], 0.0), writes=kbb)
                        P.op("dve", _mk("memset", VG[:, :, :, :], 1.0), writes=vgb)
                        pei = [0]
                        rdi = [0]
                        for j in range(NP):
                            P.op("sp", _mk("dma_start",
                                out=bst[:, :, :, :], in_=nab[sl][2 * j:2 * j + 2].rearrange("h t k q -> k h t q")),
                                 writes=[b_bst], sem=bsem)
                            P.op("act", _mk("activation", out=Et[:, :, :, :], in_=bst[:, :, :, :], func=AF.Exp),
                                 reads=[b_bst], writes=[b_et])
                            rt, rb = ring_get(e["qkv"][j])
                            for ti, (wh, c0, n, hl, hr) in enumerate(tiles):
                                if wh == 0 or ctx_out:
                                    pt, pb = next_ps()
                                    for kc in range(DC):
                                        P.op("pe", _mk("matmul",
                                            pt[:, 0:n], lhsT=rt[:, kc * 384: kc * 384 + 128], rhs=aR[:, kc, c0:c0 + n],
                                            start=(kc == 0), stop=(kc == DC - 1)), reads=[rb, aRb[ti][kc]], writes=[pb])
                                    P.op("act", _mk("activation",
                                        out=QT[:, c0:c0 + n], in_=pt[:, 0:n], func=AF.Copy), reads=[pb], writes=[qtb[ti]])
                                pt, pb = next_ps()
                                for kc in range(DC):
                                    P.op("pe", _mk("matmul",
                                        pt[:, 0:n], lhsT=rt[:, kc * 384 + 128: kc * 384 + 256], rhs=aR[:, kc, c0:c0 + n],
                                        start=(kc == 0), stop=(kc == DC - 1)), reads=[rb, aRb[ti][kc]], writes=[pb])
                                P.op("act", _mk("activation",
                                    out=KA[0:64, c0:c0 + n], in_=pt[0:64, 0:n], func=AF.Copy), reads=[pb], writes=[kab[ti]])
                                P.op("dve", _mk("tensor_copy",
                                    out=KB[64:128, c0:c0 + n], in_=pt[64:128, 0:n]), reads=[pb], writes=[kbb[ti]])
                            tile_of_blk = []
                            for ti, (wh, c0, n, hl, hr) in enumerate(tiles):
                                assert c0 % 128 == 0 and n % 128 == 0
                                tile_of_blk += [ti] * (n // 128)
                            for g0 in range(0, NB, 4):
                                nb = min(4, NB - g0)
                                pt, pb = next_ps()
                                for b in range(nb):
                                    blk = g0 + b
                                    for kc in range(DC):
                                        P.op("pe", _mk("matmul",
                                            pt[:, b * 128:(b + 1) * 128], lhsT=aR[:, kc, blk * 128:(blk + 1) * 128],
                                            rhs=rt[:, kc * 384 + 256: kc * 384 + 384],
                                            start=(kc == 0), stop=(kc == DC - 1)),
                                             reads=[rb, aRb[tile_of_blk[blk]][kc]], writes=[pb])
                                pv3 = pt[:, 0:nb * 128].rearrange("p (b c) -> p b c", c=128)
                                P.op("dve", _mk("tensor_copy",
                                    out=VG[:, g0:g0 + nb, 0, 0:64], in_=pv3[:, :, 0:64]), reads=[pb], writes=vgb[g0:g0 + nb])
                                P.op("act", _mk("activation",
                                    out=VG[:, g0:g0 + nb, 1, 64:128], in_=pv3[:, :, 64:128], func=AF.Copy),
                                     reads=[pb], writes=vgb[g0:g0 + nb])
                            def qb_tile(c):
                                for ti, (wh, c0, n, hl, hr) in enumerate(tiles):
                                    if c0 <= c < c0 + n:
                                        return ti
                            ctxblks = list(range(T // 128, NB))

                            astb = {}

                            def stage1(q0, nq, kblks, head, tidx):
                                Kt, ktb = (KA, kab) if head == 0 else (KB, kbb)
                                nblk = len(kblks)
                                tot = nblk * nq
                                sp_ = [next_ps() for _ in range(-(-tot // 512))]
                                per = 512 // nq
                                qti = qb_tile(q0)
                                for i, kb in enumerate(kblks):
                                    pt, pb = sp_[i // per]
                                    o = (i % per) * nq
                                    P.op("pe", _mk("matmul",
                                        pt[:, o:o + nq], lhsT=Kt[:, kb * 128:(kb + 1) * 128], rhs=QT[:, q0:q0 + nq],
                                        start=True, stop=True),
                                         reads=[ktb[tile_of_blk[kb]], qtb[qti]], writes=[pb])
                                ps_ = pei[0] % 4
                                pei[0] += 1
                                for bi, (pt, pb) in enumerate(sp_):
                                    cw = min(512, tot - bi * 512)
                                    P.op("act", _mk("activation",
                                        out=pe_[:, ps_, bi * 512: bi * 512 + cw], in_=pt[:, 0:cw], func=AF.Exp, scale=0.125),
                                         reads=[pb], writes=[peb[ps_]])
                                if tidx is not None:
                                    nl = nblk - len(ctxblks)
                                    P.op("dve" if head == 0 else "pool", _mk("tensor_tensor",
                                        out=pe_[:, ps_, 0:nl * 128].rearrange("p (a b) -> p a b", b=128),
                                        in0=pe_[:, ps_, 0:nl * 128].rearrange("p (a b) -> p a b", b=128),
                                        in1=Et[:, head, tidx:tidx + nl, :], op=ALU.mult),
                                         reads=[peb[ps_], b_et], writes=[peb[ps_]])
                                return (q0, nq, kblks, head, ps_)

                            def stage2(state):
                                q0, nq, kblks, head, ps_ = state
                                nblk = len(kblks)
                                po, pob = next_ps()
                                for i, kb in enumerate(kblks):
                                    P.op("pe", _mk("matmul",
                                        po[:, 0:nq], lhsT=VG[:, kb, head, :], rhs=pe_[:, ps_, i * nq:(i + 1) * nq],
                                        start=(i == 0), stop=(i == nblk - 1)), reads=[vgb[kb], peb[ps_]], writes=[pob])
                                r = rdi[0] % 4
                                rdi[0] += 1
                                dlo, olo = (64, 0) if head == 0 else (0, 64)
                                ab = astb.setdefault((q0, head), Buf())
                                P.op("dve", _mk("reciprocal", out=rd[dlo:dlo + 64, r, 0:nq], in_=po[dlo:dlo + 64, 0:nq]),
                                     reads=[pob], writes=[rdb[r]])
                                P.op("dve", _mk("tensor_tensor",
                                    out=ast[olo:olo + 64, q0:q0 + nq], in0=po[olo:olo + 64, 0:nq],
                                    in1=rd[dlo:dlo + 64, r, 0:nq], op=ALU.mult), reads=[pob, rdb[r]], writes=[ab])

                            items = []
                            for qb in range(cfg.NQB):
                                kb0, nk, t0 = qb_plan(cfg, qb)
                                for head in range(2):
                                    items.append((qb * 128, 128, list(range(kb0, kb0 + nk)) + ctxblks, head, t0))
                            if ctx_out:
                                for head in range(2):
                                    items.append((T, C, ctxblks, head, None))
                            for it in items:
                                ab = astb.setdefault((it[0], it[3]), Buf())
                                ab.r = list(ast_readers)
                            DEPTH_AHEAD = 2
                            pend = []
                            for it in items:
                                pend.append(stage1(*it))
                                if len(pend) > DEPTH_AHEAD:
                                    stage2(pend.pop(0))
                            while pend:
                                stage2(pend.pop(0))
                            ncol = TT if ctx_out else T
                            ast_readers = [P.op("sp", _mk("dma_start",
                                out=ATd[j * 128:(j + 1) * 128, 0:ncol], in_=ast[:, 0:ncol]),
                                reads=list(astb.values()), sem=asem)]
                        P.barrier()
                    st, xts, xtb, mT, mTb = alloc_common(last)
                    st["hT"] = lsb("hT", [128, FC, 512], BF16)
                    st["hb"] = [Buf() for _ in range(FC)]
                    att = lsb("att", [128, 2, DC, 512], BF16)
                    attb = [[Buf() for _ in range(DC)] for _ in range(2)]
                    at_sem = [P.new_sem(f"at{l}_0"), P.new_sem(f"at{l}_1")]
                    prep_next()
                    tiles3 = make_tiles(0, ctx_out)

                    def load3(ti):
                        wh, c0, n, hl, hr = tiles3[ti]
                        load_x(ti % 2, c0, n, 0)
                        P.op("sp", _mk("dma_start",
                            out=att[:, ti % 2, :, 0:n], in_=ATd.rearrange("(k p) t -> p k t", p=128)[:, :, c0:c0 + n]),
                             writes=attb[ti % 2], sem=at_sem[ti % 2])
                    load3(0)
                    for ti, (wh, c0, n, hl, hr) in enumerate(tiles3):
                        slot = ti % 2
                        if ti + 1 < len(tiles3):
                            load3(ti + 1)
                        proj_kd(e["wo"], lambda kc: att[:, slot, kc, 0:n], attb[slot], n,
                                resid_evac(l, wh, xts[:, slot], xtb[slot], 0, n))
                        tail(wh, slot, c0, n, 0, ti, len(tiles3))
                    finish_next()
                P.barrier()
            src = dst

        with nc.Block() as block:
            P.lower(block)
    return nc


_CACHE = {}


def prepare_inputs(cfg, inp):
    vecs, voff = build_vecs(cfg, inp)
    nab = np.stack([build_bias_tiles(np.asarray(inp["na_rpb"][s], np.float32)) for s in range(cfg.n_na)])
    shared = {
        "vecs": vecs, "nab": nab,
        "mod_w": np.ascontiguousarray(inp["mod_w"], np.float32),
        "mlp_w1": np.ascontiguousarray(inp["mlp_w1"], np.float32),
        "mlp_w2": np.ascontiguousarray(inp["mlp_w2"], np.float32),
        "na_wqkv": np.ascontiguousarray(inp["na_wqkv"], np.float32),
        "na_wo": np.ascontiguousarray(inp["na_wo"], np.float32),
        "cv_w1": np.ascontiguousarray(inp["cv_w1"], np.float32),
        "cv_w2": np.ascontiguousarray(inp["cv_w2"], np.float32),
        "sc_win": np.ascontiguousarray(inp["sc_win"], np.float32),
        "sc_wout": np.ascontiguousarray(inp["sc_wout"], np.float32),
    }
    in_maps = []
    for b in range(np.asarray(inp["x"]).shape[0]):
        m = dict(shared)
        m["xin"] = np.ascontiguousarray(
            np.concatenate([np.asarray(inp["x"][b], np.float32).T, np.asarray(inp["ctx"][b], np.float32).T], axis=1))
        m["cc"] = np.ascontiguousarray(
            np.stack([np.asarray(inp["c"][b], np.float32), np.asarray(inp["c_ctx"], np.float32)], axis=1))
        in_maps.append(m)
    return in_maps, voff, vecs.shape[1]


def kernel(**inputs):
    B, T, D = inputs["x"].shape
    C = inputs["ctx"].shape[1]
    depth = inputs["mod_w"].shape[0]
    cfg = Cfg(D=D, T=T, C=C, depth=depth, ncores=B)
    in_maps, voff, nvec = prepare_inputs(cfg, inputs)
    key = (D, T, C, depth)
    if key not in _CACHE:
        _CACHE[key] = build_program(cfg, voff, nvec)
    nc = _CACHE[key]
    res = run_bass_kernel_spmd(nc, in_maps, core_ids=list(range(B)))
    out = np.stack([np.ascontiguousarray(r["outT"].T) for r in res.results], axis=0)
    return out.astype(np.float32)
```
